# Optimizing a Trainium2 kernel written in Bass

```python
import math
import jax, jax.numpy as jnp
from jax import lax
import numpy as np

D_MODEL = 1024
BATCH = 4
SEQ = 8192
DEPTH = 2

D_MIX = 2 * D_MODEL
SSD_INNER = D_MIX // 2
SSD_HEAD_DIM = 64
SSD_HEADS = SSD_INNER // SSD_HEAD_DIM
SSD_GROUPS = 2
SSD_HEADS_PER_GROUP = SSD_HEADS // SSD_GROUPS
SSD_STATE = 128
CONV_WIDTH = 4
CONV_DIM = SSD_INNER + 2 * SSD_GROUPS * SSD_STATE
CHUNK = 128
POOL_WINDOWS = (2, 4, 8, 16)
POOL_WIDTH = D_MIX - SSD_INNER
POOL_GROUP_DIM = POOL_WIDTH // len(POOL_WINDOWS)
IN_PROJ_DIM = SSD_INNER + CONV_DIM + SSD_HEADS + POOL_WIDTH
MEM_LEN = 256
XATTN_HEADS = 4
XATTN_HEAD_DIM = D_MODEL // XATTN_HEADS
D_FF = int(math.ceil((8 * D_MODEL / 3) / 256) * 256)
EPS = 1e-6

kernel_name = "hybrid_ssd_pool_xattn_block"


def rmsnorm(x, g):
    xf = x.astype(jnp.float32)
    y = xf * lax.rsqrt(jnp.mean(xf * xf, axis=-1, keepdims=True) + EPS)
    return (y * g.astype(jnp.float32)).astype(x.dtype)


def causal_depthwise_conv(u, w, b):
    k, c = w.shape
    out = lax.conv_general_dilated(
        u, w[:, None, :].astype(u.dtype), window_strides=(1,), padding=[(k - 1, 0)],
        dimension_numbers=("NWC", "WIO", "NWC"), feature_group_count=c)
    return out + b.astype(u.dtype)


def ssd_chunked(xs, dt, a, bm, cm):
    bsz, s, _, p = xs.shape
    nc = s // CHUNK
    g, hg, n = SSD_GROUPS, SSD_HEADS_PER_GROUP, SSD_STATE
    xc = (xs * dt[..., None]).reshape(bsz, nc, CHUNK, g, hg, p)
    adt = (dt * a).reshape(bsz, nc, CHUNK, g, hg).transpose(0, 3, 4, 1, 2)
    bc = bm.reshape(bsz, nc, CHUNK, g, n)
    cc = cm.reshape(bsz, nc, CHUNK, g, n)
    a_cs = jnp.cumsum(adt, axis=-1)
    causal = jnp.tril(jnp.ones((CHUNK, CHUNK), dtype=bool))
    seg = jnp.exp(jnp.where(causal, a_cs[..., :, None] - a_cs[..., None, :], -jnp.inf))
    cb = jnp.einsum("bclgn,bcsgn->bgcls", cc, bc)
    y_diag = jnp.einsum("bghcls,bcsghp->bclghp", cb[:, :, None] * seg, xc)
    decay_states = jnp.exp(a_cs[..., -1:] - a_cs)
    states = jnp.einsum("bcsgn,bghcs,bcsghp->bcghpn", bc, decay_states, xc)
    chunk_decay = jnp.exp(a_cs[..., -1])

    def step(carry, inp):
        st, dec = inp
        return dec[..., None, None] * carry + st, carry

    init = jnp.zeros((bsz, g, hg, p, n), jnp.float32)
    _, prev = lax.scan(step, init, (jnp.moveaxis(states, 1, 0), jnp.moveaxis(chunk_decay, -1, 0)))
    y_off = jnp.einsum("bclgn,cbghpn,bghcl->bclghp", cc, prev, jnp.exp(a_cs))
    return (y_diag + y_off).reshape(bsz, s, SSD_HEADS, p)


def multi_scale_pool(v, pool_w, pool_scale):
    bsz, s, _ = v.shape
    vg = v.astype(jnp.float32).reshape(bsz, s, len(POOL_WINDOWS), POOL_GROUP_DIM)
    cs0 = jnp.concatenate([jnp.zeros_like(vg[:, :1]), jnp.cumsum(vg, axis=1)], axis=1)
    pos1 = jnp.arange(1, s + 1, dtype=jnp.float32)
    outs = []
    for gi, w in enumerate(POOL_WINDOWS):
        c = cs0[:, :, gi]
        lower = jnp.pad(c[:, :s + 1 - w], ((0, 0), (w - 1, 0), (0, 0)))
        count = jnp.minimum(pos1, float(w))[None, :, None]
        outs.append((c[:, 1:] - lower) / count - vg[:, :, gi])
    d = jnp.stack(outs, axis=2)
    out = jnp.einsum("bsgc,gcd->bsgd", d, pool_w.astype(jnp.float32)).reshape(bsz, s, POOL_WIDTH)
    return (out * pool_scale.astype(jnp.float32)).astype(v.dtype)


def hybrid_mixer(u, w_in, conv_w, conv_b, dt_bias, a_log, d_skip, ssd_norm, pool_w, pool_scale, w_out):
    bsz, s, _ = u.shape
    proj = u @ w_in
    z, xbc, dt_raw, v = jnp.split(
        proj, [SSD_INNER, SSD_INNER + CONV_DIM, SSD_INNER + CONV_DIM + SSD_HEADS], axis=-1)
    xbc = jax.nn.silu(causal_depthwise_conv(xbc, conv_w, conv_b)).astype(jnp.float32)
    xs, bm, cm = jnp.split(xbc, [SSD_INNER, SSD_INNER + SSD_GROUPS * SSD_STATE], axis=-1)
    xs = xs.reshape(bsz, s, SSD_HEADS, SSD_HEAD_DIM)
    bm = bm.reshape(bsz, s, SSD_GROUPS, SSD_STATE)
    cm = cm.reshape(bsz, s, SSD_GROUPS, SSD_STATE)
    dt = jax.nn.softplus(dt_raw.astype(jnp.float32) + dt_bias.astype(jnp.float32))
    a = -jnp.exp(a_log.astype(jnp.float32))
    y = ssd_chunked(xs, dt, a, bm, cm)
    y = y + d_skip.astype(jnp.float32)[:, None] * xs
    y = y.reshape(bsz, s, SSD_INNER) * jax.nn.silu(z.astype(jnp.float32))
    y_ssd = rmsnorm(y, ssd_norm).astype(u.dtype)
    y_pool = multi_scale_pool(v, pool_w, pool_scale)
    return jnp.concatenate([y_ssd, y_pool], axis=-1) @ w_out


def memory_cross_attention(q_in, m_in, w_q, w_kv, w_o):
    bsz, s, _ = q_in.shape
    q = (q_in @ w_q).reshape(bsz, s, XATTN_HEADS, XATTN_HEAD_DIM)
    k, v = jnp.split(m_in @ w_kv, 2, axis=-1)
    k = k.reshape(bsz, -1, XATTN_HEADS, XATTN_HEAD_DIM)
    v = v.reshape(bsz, -1, XATTN_HEADS, XATTN_HEAD_DIM)
    scores = jnp.einsum("bshd,bmhd->bhsm", q.astype(jnp.float32), k.astype(jnp.float32))
    probs = jax.nn.softmax(scores * (XATTN_HEAD_DIM ** -0.5), axis=-1).astype(v.dtype)
    o = jnp.einsum("bhsm,bmhd->bshd", probs, v).reshape(bsz, s, D_MODEL)
    return o @ w_o


def swiglu(u, w_gate_up, w_down):
    gate, up = jnp.split(u @ w_gate_up, 2, axis=-1)
    return (jax.nn.silu(gate) * up) @ w_down


def setup_inputs(seed: int = 0) -> dict:
    key = jax.random.key(seed)
    ks = iter(jax.random.split(key, 32))
    f32 = jnp.float32

    def nrm(shape, fan_in):
        return jax.random.normal(next(ks), shape, f32) * (fan_in ** -0.5)

    def gain(shape):
        return 1.0 + 0.02 * jax.random.normal(next(ks), shape, f32)

    L = DEPTH
    x = jax.random.normal(next(ks), (BATCH, SEQ, D_MODEL), f32)
    mem = jax.random.normal(next(ks), (BATCH, MEM_LEN, D_MODEL), f32)
    dt0 = jnp.exp(jax.random.uniform(next(ks), (L, SSD_HEADS), f32, math.log(1e-3), math.log(1e-1)))
    dt_bias = dt0 + jnp.log(-jnp.expm1(-dt0))
    a_log = jnp.log(jax.random.uniform(next(ks), (L, SSD_HEADS), f32, 1.0, 16.0))
    return {
        "x": x,
        "mem": mem,
        "mix_norm": gain((L, D_MODEL)),
        "w_in": nrm((L, D_MODEL, IN_PROJ_DIM), D_MODEL),
        "conv_w": nrm((L, CONV_WIDTH, CONV_DIM), CONV_WIDTH),
        "conv_b": 0.02 * jax.random.normal(next(ks), (L, CONV_DIM), f32),
        "dt_bias": dt_bias,
        "a_log": a_log,
        "d_skip": gain((L, SSD_HEADS)),
        "ssd_norm": gain((L, SSD_INNER)),
        "pool_w": nrm((L, len(POOL_WINDOWS), POOL_GROUP_DIM, POOL_GROUP_DIM), POOL_GROUP_DIM),
        "pool_scale": gain((L, POOL_WIDTH)),
        "w_out_mix": nrm((L, D_MIX, D_MODEL), D_MIX),
        "xattn_norm": gain((L, D_MODEL)),
        "mem_norm": gain((L, D_MODEL)),
        "w_q": nrm((L, D_MODEL, D_MODEL), D_MODEL),
        "w_kv": nrm((L, D_MODEL, 2 * D_MODEL), D_MODEL),
        "w_o": nrm((L, D_MODEL, D_MODEL), D_MODEL),
        "ffn_norm": gain((L, D_MODEL)),
        "w_gate_up": nrm((L, D_MODEL, 2 * D_FF), D_MODEL),
        "w_down": nrm((L, D_FF, D_MODEL), D_FF),
        "final_norm": gain((D_MODEL,)),
    }


def reference(x, mem, mix_norm, w_in, conv_w, conv_b, dt_bias, a_log, d_skip, ssd_norm,
              pool_w, pool_scale, w_out_mix, xattn_norm, mem_norm, w_q, w_kv, w_o,
              ffn_norm, w_gate_up, w_down, final_norm):
    h = x
    for l in range(DEPTH):
        h = h + hybrid_mixer(rmsnorm(h, mix_norm[l]), w_in[l], conv_w[l], conv_b[l], dt_bias[l],
                             a_log[l], d_skip[l], ssd_norm[l], pool_w[l], pool_scale[l], w_out_mix[l])
        h = h + memory_cross_attention(rmsnorm(h, xattn_norm[l]), rmsnorm(mem, mem_norm[l]),
                                       w_q[l], w_kv[l], w_o[l])
        h = h + swiglu(rmsnorm(h, ffn_norm[l]), w_gate_up[l], w_down[l])
    return rmsnorm(h, final_norm)
```

```python
import numpy as np
import concourse.bass as bass
import concourse.mybir as mybir
from concourse.bass_utils import run_bass_kernel_spmd
from contextlib import ExitStack

F32 = mybir.dt.float32
BF16 = mybir.dt.bfloat16
AF = mybir.ActivationFunctionType
ALU = mybir.AluOpType

DEPTH = 2
SEQ = 8192
EPS = 1e-6
NPV = 116
import os
DBG = int(os.environ.get('KDBG', '99'))


class Buf:
    __slots__ = ("name", "last_w", "readers", "sem", "semcnt")

    def __init__(self, name):
        self.name = name
        self.last_w = None
        self.readers = []
        self.sem = None
        self.semcnt = 0


class FW:
    def __init__(self, nc, es):
        self.nc = nc
        self.es = es
        self.engs = {"pe": nc.tensor, "dve": nc.vector, "act": nc.scalar,
                     "pool": nc.gpsimd, "sp": nc.sync}
        self.sems = {}
        self.cnt = {}
        for k in self.engs:
            self.sems[k] = es.enter_context(nc.semaphore("s_" + k))
            self.cnt[k] = 0
        self.seen = {k: {} for k in self.engs}
        self.nsem = len(self.engs)
        self.nwaits = 0
        self.nops = 0
        self.dmabufs = []

    def _wait(self, eng, deps):
        e = self.engs[eng]
        best = {}
        for (k, v) in deps:
            if k == eng and eng == "pe":
                continue
            if best.get(k, 0) < v:
                best[k] = v
        for k, v in best.items():
            if self.seen[eng].get(k, 0) < v:
                e.wait_ge(self.sems[k], v)
                self.seen[eng][k] = v
                self.nwaits += 1

    def _deps(self, reads, writes):
        deps = []
        for b in reads:
            if b.last_w is not None:
                deps.append(b.last_w)
        for b in writes:
            if b.last_w is not None:
                deps.append(b.last_w)
            deps.extend(b.readers)
        return deps

    def _record(self, ident, reads, writes):
        for b in reads:
            b.readers.append(ident)
            if len(b.readers) > 48:
                best = {}
                for k, v in b.readers:
                    if best.get(k, 0) < v:
                        best[k] = v
                b.readers = list(best.items())
        for b in writes:
            b.last_w = ident
            b.readers = []

    def op(self, eng, fn, reads=(), writes=(), inc=True):
        self._wait(eng, self._deps(reads, writes))
        inst = fn()
        self.nops += 1
        if inc:
            self.cnt[eng] += 1
            inst.then_inc(self.sems[eng], 1)
            ident = (eng, self.cnt[eng])
        else:
            ident = (eng, self.cnt[eng] + 1)
        self._record(ident, reads, writes)
        return inst

    def dma(self, eng, out, in_, reads=(), writes=(), sembuf=None, **kw):
        if sembuf.sem is None:
            key = "d%d" % self.nsem
            sembuf.sem = key
            self.dmabufs.append(sembuf)
            self.sems[key] = self.es.enter_context(self.nc.semaphore(key))
            self.nsem += 1
        self._wait(eng, self._deps(reads, writes))
        inst = self.engs[eng].dma_start(out=out, in_=in_, **kw)
        sembuf.semcnt += 16
        inst.then_inc(self.sems[sembuf.sem], 16)
        ident = (sembuf.sem, sembuf.semcnt)
        self._record(ident, reads, writes)
        return inst

    def barrier(self):
        deps = [(k, self.cnt[k]) for k in self.engs if self.cnt[k] > 0]
        deps += [(b.sem, b.semcnt) for b in self.dmabufs]
        for eng in self.engs:
            self._wait(eng, deps)

    def finish(self, bufs):
        deps = []
        for b in bufs:
            if b.last_w is not None:
                deps.append(b.last_w)
            deps.extend(b.readers)
        self._wait("sp", deps)


class Rot:
    def __init__(self, n):
        self.n = n
        self.p = 0

    def get(self, k=1):
        if self.p + k > self.n:
            self.p = 0
        r = self.p
        self.p += k
        if self.p >= self.n:
            self.p = 0
        return r


def build(ntok, depth, stop_after=None):
    nc = bass.Bass("TRN2", target_bir_lowering=False)
    NCH = ntok // 128
    NT = ntok // 512

    def din(name, shape):
        return nc.dram_tensor(name, shape, F32, kind="ExternalInput").ap()

    x = din("x", [ntok, 1024])
    mem = din("mem", [256, 1024])
    w_in = din("w_in", [depth, 1024, 3600])
    pool_w = din("pool_w", [depth, 4, 256, 256])
    w_out = din("w_out_mix", [depth, 2048, 1024])
    w_q = din("w_q", [depth, 1024, 1024])
    w_kv = din("w_kv", [depth, 1024, 2048])
    w_o = din("w_o", [depth, 1024, 1024])
    w_gu = din("w_gate_up", [depth, 1024, 5632])
    w_dn = din("w_down", [depth, 2816, 1024])
    pvec = din("pvec", [depth, 128, NPV])
    rowp = din("rowp", [depth, 3, 16])
    fnorm = din("final_norm", [1024])
    cmat = din("cmat", [17, 128, 128])
    out = nc.dram_tensor("out", [ntok, 1024], F32, kind="ExternalOutput").ap()
    hbuf = nc.dram_tensor("hbuf", [ntok, 1024], F32).ap()

    with ExitStack() as es:
        fw = FW(nc, es)

        uniq = [0]

        def sb(st, name, shape, dt):
            uniq[0] += 1
            return st.enter_context(nc.sbuf_tensor("%s_%d" % (name, uniq[0]), shape, dt))

        def MM(out_, lhsT, rhs, st, sp, R, W, inc=False):
            fw.op("pe", lambda: nc.tensor.matmul(out_, lhsT, rhs, start=st, stop=sp), R, W, inc=inc)

        def TR(out_, in_, idn, R, W, inc=False):
            fw.op("pe", lambda: nc.tensor.transpose(out_, in_, idn), R, W, inc=inc)

        def ACT(out_, in_, func, R, W, **kw):
            fw.op("act", lambda: nc.scalar.activation(out=out_, in_=in_, func=func, **kw), R, W)

        def TT(eng, out_, in0, in1, op, R, W):
            e = fw.engs[eng]
            fw.op(eng, lambda: e.tensor_tensor(out=out_, in0=in0, in1=in1, op=op), R, W)

        def TS(eng, out_, in0, s1, op0, R, W):
            e = fw.engs[eng]
            fw.op(eng, lambda: e.tensor_scalar(out=out_, in0=in0, scalar1=s1, scalar2=None, op0=op0), R, W)

        def STT(eng, out_, in0, scalar, in1, op0, op1, R, W):
            e = fw.engs[eng]
            fw.op(eng, lambda: e.scalar_tensor_tensor(out=out_, in0=in0, scalar=scalar, in1=in1, op0=op0, op1=op1), R, W)

        def CP(eng, out_, in_, R, W):
            e = fw.engs[eng]
            fw.op(eng, lambda: e.tensor_copy(out=out_, in_=in_), R, W)

        def MS(eng, ap, val, W):
            e = fw.engs[eng]
            fw.op(eng, lambda: e.memset(ap, val), (), W)

        pf = es.enter_context(nc.psum_tensor("pf", [128, 6 * 512], F32))
        pb = es.enter_context(nc.psum_tensor("pb", [128, 2 * 1024], BF16))
        Bpf = [Buf("pf%d" % i) for i in range(6)]
        Bpb = [Buf("pb%d" % i) for i in range(2)]
        rotf = Rot(6)
        rotb = Rot(2)

        def bank(i, n=1):
            return pf[:, i * 512:(i + n) * 512]

        def bbank(i):
            return pb[:, i * 1024:(i + 1) * 1024]

        cst = es
        ident_b = sb(cst, "ident_b", [128, 128], BF16)
        ones_b = sb(cst, "ones_b", [128, 128], BF16)
        pmat_b = sb(cst, "pmat_b", [128, 12, 128], BF16)
        cf = sb(cst, "cf", [128, 3, 128], F32)
        fn_b = sb(cst, "fn_b", [128, 1024], F32)
        Bc = Buf("consts")
        fw.dma("pool", ident_b[:], cmat[0], writes=[Bc], sembuf=Bc)
        fw.dma("pool", ones_b[:], cmat[3], writes=[Bc], sembuf=Bc)
        for i in range(12):
            fw.dma("pool", pmat_b[:, i, :], cmat[4 + i], writes=[Bc], sembuf=Bc)
        Bcf = Buf("constsf")
        for i in range(3):
            fw.dma("sp", cf[:, i, :], cmat[1 + i], writes=[Bcf], sembuf=Bcf)
        fw.dma("sp", fn_b[:], fnorm.partition_broadcast(128), writes=[Bcf], sembuf=Bcf)
        Umat = cf[:, 0, :]
        tri = cf[:, 1, :]
        onesf = cf[:, 2, :]

        pv = sb(cst, "pv", [128, depth, NPV], F32)
        rp = sb(cst, "rp", [128, depth, 48], F32)
        Bpv = Buf("pv")
        for l in range(depth):
            fw.dma("sp", pv[:, l, :], pvec[l], writes=[Bpv], sembuf=Bpv)
            for i in range(3):
                fw.dma("sp", rp[:, l, i * 16:(i + 1) * 16], rowp[l, i].partition_broadcast(128), writes=[Bpv], sembuf=Bpv)
            ACT(rp[:, l, 16:32], rp[:, l, 16:32], AF.Exp, [Bpv], [Bpv])
            TS("dve", rp[:, l, 16:32], rp[:, l, 16:32], -1.0, ALU.mult, [Bpv], [Bpv])
        PV_MIX, PV_XAT, PV_FFN, PV_MEM, PV_SSD, PV_CW, PV_CB, PV_PS = 0, 8, 16, 24, 32, 40, 88, 100

        hs = sb(cst, "hs", [128, 4, 1024], F32)
        Bhs = [Buf("hs%d" % i) for i in range(4)]
        Bh = [Buf("h%d" % c) for c in range(NCH)]
        ub2 = sb(cst, "ub", [128, 2, 1024], BF16)
        Bub2 = [Buf("ub0"), Buf("ub1")]
        rotu = Rot(2)
        junk = sb(cst, "junk", [128, 1024], BF16)
        Bjunk = Buf("junk")
        ssr = sb(cst, "ssr", [128, 4, 4], F32)
        Bss = [Buf("ss%d" % i) for i in range(4)]
        rots = Rot(4)
        uT = sb(cst, "uT", [128, 8, 512], BF16)
        BuT = Buf("uT")

        def rstd_of(src, Bsrc):
            i = rots.get()
            s = ssr[:, i, :]
            MS("pool", s[:, 0:1], 0.0, [Bss[i]])
            ACT(junk[:], src, AF.Square, [Bsrc, Bss[i]], [Bjunk, Bss[i]], accum_out=s[:, 0:1])
            ACT(s[:, 1:2], s[:, 0:1], AF.Ln, [Bss[i]], [Bss[i]], scale=1.0 / 1024, bias=EPS)
            ACT(s[:, 2:3], s[:, 1:2], AF.Exp, [Bss[i]], [Bss[i]], scale=-0.5)
            return s[:, 2:3], Bss[i]

        def norm_transpose(src, Bsrc, gT, dst, Bdst, tokoff):
            r, Br = rstd_of(src, Bsrc)
            u = rotu.get()
            ACT(ub2[:, u, :], src, AF.Identity, [Bsrc, Br], [Bub2[u]], scale=r)
            transpose8(ub2[:, u, :], Bub2[u], gT, dst, Bdst, tokoff)

        def transpose8(srcb, Bsrcb, gT, dst, Bdst, tokoff):
            i = rotb.get()
            for k in range(8):
                TR(bbank(i)[:, k * 128:(k + 1) * 128], srcb[:, k * 128:(k + 1) * 128], ident_b[:],
                   [Bsrcb, Bc], [Bpb[i]], inc=(k == 7))
            TT("dve", dst[:, 0:8, tokoff:tokoff + 128],
               bbank(i).rearrange("p (q t) -> p q t", t=128),
               gT.unsqueeze(2).to_broadcast([128, 8, 128]),
               ALU.mult, [Bpb[i], Bpv], [Bdst])

        def load_h(src, c):
            j = c % 4
            fw.dma("sp", hs[:, j, :], src[c * 128:(c + 1) * 128, :], reads=[Bh[c]], writes=[Bhs[j]], sembuf=Bhs[j])

        def store_h(dst, c, srcap=None, Bsrc=None):
            j = c % 4
            if srcap is None:
                srcap, Bsrc = hs[:, j, :], Bhs[j]
            fw.dma("sp", dst[c * 128:(c + 1) * 128, :], srcap, reads=[Bsrc], writes=[Bh[c]], sembuf=Bsrc)

        def load_w_cols(dst3, src2, nk, blocks):
            for Bb, ranges in blocks:
                for k in range(nk):
                    for (c0, c1) in ranges:
                        fw.dma("pool", dst3[:, k, c0:c1], src2[k * 128:(k + 1) * 128, c0:c1], writes=[Bb], sembuf=Bb)

        def load_w(dst3, Bdst, src2, nk):
            for k in range(nk):
                if os.environ.get('KNOW'):
                    continue
                fw.dma("pool", dst3[:, k, :], src2[k * 128:(k + 1) * 128, :], writes=[Bdst], sembuf=Bdst)

        def mixer(l, src):
            with ExitStack() as ph:
                w_in_sb = sb(ph, "w_in_sb", [128, 8, 3600], BF16)
                w_out_sb = sb(ph, "w_out_sb", [128, 16, 1024], BF16)
                pool_sb = sb(ph, "pool_sb", [128, 8, 256], BF16)
                Bwout, Bpool = Buf("w_out"), Buf("poolw")
                Bwin_x, Bwin_z, Bwin_v = Buf("w_in_x"), Buf("w_in_z"), Buf("w_in_v")
                for c in range(min(4, NCH)):
                    load_h(src, c)
                for k in range(8):
                    fw.dma("pool", w_in_sb[:, k, :], w_in[l, k * 128:(k + 1) * 128, :],
                           writes=[Bwin_x, Bwin_z, Bwin_v], sembuf=Bwin_x)
                for g in range(4):
                    for cc in range(2):
                        fw.dma("pool", pool_sb[:, g * 2 + cc, :], pool_w[l, g, cc * 128:(cc + 1) * 128, :],
                               writes=[Bpool], sembuf=Bpool)
                load_w(w_out_sb, Bwout, w_out[l], 16)

                raw = sb(ph, "raw", [128, 2, 516], F32)
                Braw = [Buf("raw0"), Buf("raw1")]
                acc = sb(ph, "acc", [128, 2, 512], F32)
                Bacc = [Buf("acc0"), Buf("acc1")]
                halo = sb(ph, "halo", [128, 12, 3], F32)
                Bhalo = Buf("halo")
                xbcT = sb(ph, "xbcT", [128, 12, 512], BF16)
                BxT = Buf("xbcT")
                zs = sb(ph, "zs", [128, 1024], BF16)
                Bzs = Buf("zs")
                vt = sb(ph, "vt", [128, 2, 1024], BF16)
                Bvt = [Buf("vt0"), Buf("vt1")]
                xs_tok = sb(ph, "xs_tok", [128, 1024], BF16)
                Bxs = Buf("xs_tok")
                B_tok = sb(ph, "B_tok", [128, 256], BF16)
                BBt = Buf("B_tok")
                sm = sb(ph, "sm", [128, 8, 16], F32)
                Bsm = Buf("sm")
                Wm = sb(ph, "Wm", [128, 16, 128], F32)
                BWm = Buf("Wm")
                E = sb(ph, "E", [128, 16, 128], BF16)
                BE = Buf("E")
                Mm = sb(ph, "Mm", [128, 16, 128], BF16)
                BM = Buf("Mm")
                CBm = sb(ph, "CBm", [128, 2, 128], F32)
                BCBm = Buf("CBm")
                xdt = sb(ph, "xdt", [128, 1024], BF16)
                Bxdt = Buf("xdt")
                xdd = sb(ph, "xdd", [128, 1024], BF16)
                Bxdd = Buf("xdd")
                xsd = sb(ph, "xsd", [128, 1024], BF16)
                Bxsd = Buf("xsd")
                S = sb(ph, "S", [128, 1024], F32)
                BS = Buf("S")
                S_bf = sb(ph, "S_bf", [128, 1024], BF16)
                BSb = Buf("S_bf")
                yb = sb(ph, "yb", [128, 1024], F32)
                Byb = Buf("yb")
                ygn = sb(ph, "ygn", [128, 1024], BF16)
                Bygn = Buf("ygn")
                catT = sb(ph, "catT", [128, 2, 16, 128], BF16)
                Bcat = [Buf("catT0"), Buf("catT1")]
                dTs = sb(ph, "dTs", [128, 8, 128], BF16)
                BdT = Buf("dTs")

                gT = pv[:, l, PV_MIX:PV_MIX + 8]
                ssdT = pv[:, l, PV_SSD:PV_SSD + 8]
                cw = pv[:, l, PV_CW:PV_CW + 48]
                cb = pv[:, l, PV_CB:PV_CB + 12]
                psT = pv[:, l, PV_PS:PV_PS + 8]
                dtb = rp[:, l, 0:16]
                a_b = rp[:, l, 16:32]
                dsk = rp[:, l, 32:48]

                MS("pool", halo[:], 0.0, [Bhalo])
                MS("pool", S[:], 0.0, [BS])
                MS("pool", S_bf[:], 0.0, [BSb])

                def b16(ap):
                    return ap.unsqueeze(2).to_broadcast([128, 16, 64])

                def v3(ap):
                    return ap.rearrange("p (h j) -> p h j", j=64)

                def step1(j):
                    tk = slice(j * 128, (j + 1) * 128)
                    xr, ax, ex, l1, dt_, adt, dtds, ea = [sm[:, q, :] for q in range(8)]
                    i = rotf.get()
                    for k in range(8):
                        MM(bank(i)[:, 0:16], uT[:, k, tk], w_in_sb[:, k, 2560:2576],
                           k == 0, k == 7, [Bwin_v, BuT], [Bpf[i]], inc=(k == 7))
                    TT("dve", xr, bank(i)[:, 0:16], dtb, ALU.add, [Bpf[i], Bpv], [Bsm])
                    ACT(ax, xr, AF.Abs, [Bsm], [Bsm])
                    ACT(ex, ax, AF.Exp, [Bsm], [Bsm], scale=-1.0)
                    ACT(l1, ex, AF.Ln, [Bsm], [Bsm], bias=1.0)
                    STT("dve", dt_, xr, 0.0, l1, ALU.max, ALU.add, [Bsm], [Bsm])
                    TT("dve", adt, dt_, a_b, ALU.mult, [Bsm, Bpv], [Bsm])
                    TT("pool", Wm[:], tri.unsqueeze(1).to_broadcast([128, 16, 128]),
                       adt.unsqueeze(2).to_broadcast([128, 16, 128]), ALU.mult, [Bcf, Bsm], [BWm])

                pend = [None]

                def outproj(c):
                    j = c % 4
                    cbf = c % 2
                    iz = rotf.get(2)
                    for nb in range(2):
                        for k in range(16):
                            MM(bank(iz + nb), catT[:, cbf, k, :], w_out_sb[:, k, nb * 512:(nb + 1) * 512],
                               k == 0, k == 15, [Bcat[cbf], Bwout], [Bpf[iz + nb]], inc=(k == 15))
                    TT("dve", hs[:, j, :], hs[:, j, :], bank(iz, 2), ALU.add, [Bhs[j], Bpf[iz], Bpf[iz + 1]], [Bhs[j]])
                    store_h(hbuf, c)
                    if c + 4 < NCH:
                        load_h(src, c + 4)

                def flush_pending():
                    if pend[0] is not None:
                        outproj(pend[0])
                        pend[0] = None

                for t in range(NT):
                    flush_pending()
                    for j in range(4):
                        norm_transpose(hs[:, j, :], Bhs[j], gT, uT, BuT, j * 128)
                    for cc in range(12):
                        i = rotf.get()
                        for k in range(8):
                            MM(bank(i), w_in_sb[:, k, 1024 + cc * 128:1024 + (cc + 1) * 128], uT[:, k, :],
                               k == 0, k == 7, [Bwin_x, BuT], [Bpf[i]], inc=(k == 7))
                        r = cc % 2
                        CP("pool", raw[:, r, 0:3], halo[:, cc, :], [Bhalo], [Braw[r]])
                        ACT(raw[:, r, 3:515], bank(i), AF.Identity, [Bpf[i]], [Braw[r]])
                        CP("pool", halo[:, cc, :], raw[:, r, 512:515], [Braw[r]], [Bhalo])
                        ACT(acc[:, r, :], bank(i), AF.Identity, [Bpf[i], Bpv], [Bacc[r]], scale=cw[:, cc * 4 + 3:cc * 4 + 4])
                        for k in range(3):
                            STT("dve", acc[:, r, :], raw[:, r, k:k + 512], cw[:, cc * 4 + k:cc * 4 + k + 1],
                                acc[:, r, :], ALU.mult, ALU.add, [Braw[r], Bpv, Bacc[r]], [Bacc[r]])
                        ACT(xbcT[:, cc, :], acc[:, r, :], AF.Silu, [Bacc[r], Bpv], [BxT], bias=cb[:, cc:cc + 1])
                    for j in range(4):
                        c = t * 4 + j
                        tk = slice(j * 128, (j + 1) * 128)
                        cur, prv = c % 2, (c + 1) % 2
                        cbf = c % 2
                        xr, ax, ex, l1, dt_, adt, dtds, ea = [sm[:, q, :] for q in range(8)]
                        if j == 0:
                            step1(j)
                        ib = rotb.get()
                        for k in range(8):
                            TR(bbank(ib)[:, k * 128:(k + 1) * 128], xbcT[:, k, tk], ident_b[:],
                               [BxT, Bc], [Bpb[ib]], inc=(k == 7))
                        ACT(xs_tok[:], bbank(ib), AF.Identity, [Bpb[ib]], [Bxs])
                        ib = rotb.get()
                        for q in range(2):
                            TR(bbank(ib)[:, q * 128:(q + 1) * 128], xbcT[:, 8 + q, tk], ident_b[:],
                               [BxT, Bc], [Bpb[ib]], inc=(q == 1))
                        CP("dve", B_tok[:], bbank(ib)[:, 0:256], [Bpb[ib]], [BBt])
                        TT("pool", v3(xsd[:]), v3(xs_tok[:]), b16(dsk), ALU.mult, [Bxs, Bpv], [Bxsd])
                        i = rotf.get(2)
                        for nb in range(2):
                            for k in range(8):
                                MM(bank(i + nb), uT[:, k, tk], w_in_sb[:, k, nb * 512:(nb + 1) * 512],
                                   k == 0, k == 7, [Bwin_z, BuT], [Bpf[i + nb]], inc=(k == 7))
                        ACT(zs[:], bank(i, 2), AF.Silu, [Bpf[i], Bpf[i + 1]], [Bzs])
                        ic = rotf.get()
                        for g in range(2):
                            MM(bank(ic)[:, g * 128:(g + 1) * 128], xbcT[:, 8 + g, tk], xbcT[:, 10 + g, tk],
                               True, True, [BxT], [Bpf[ic]], inc=(g == 1))
                        TT("dve", CBm[:], bank(ic)[:, 0:256].rearrange("p (g l) -> p g l", g=2),
                           tri.unsqueeze(1).to_broadcast([128, 2, 128]), ALU.mult, [Bpf[ic], Bcf], [BCBm])
                        i4 = rotf.get(4)
                        for q in range(4):
                            MM(bank(i4 + q), Umat, Wm[:, q * 4:(q + 1) * 4, :].rearrange("p h l -> p (h l)"),
                               True, True, [Bcf, BWm], [Bpf[i4 + q]], inc=True)
                        ia = rotf.get()
                        MM(bank(ia)[:, 0:16], tri, adt, True, True, [Bcf, Bsm], [Bpf[ia]])
                        MM(bank(ia)[:, 16:32], onesf, adt, True, True, [Bcf, Bsm], [Bpf[ia]], inc=True)
                        for q in range(4):
                            ACT(E[:, q * 4:(q + 1) * 4, :].rearrange("p h l -> p (h l)"), bank(i4 + q), AF.Exp,
                                [Bpf[i4 + q]], [BE])
                        cdb = l1
                        ACT(ea, bank(ia)[:, 0:16], AF.Exp, [Bpf[ia]], [Bsm])
                        ACT(cdb, bank(ia)[:, 16:32], AF.Exp, [Bpf[ia]], [Bsm])
                        TT("pool", v3(S[:]), v3(S[:]), b16(cdb), ALU.mult, [BS, Bsm], [BS])
                        CP("dve", xr, bank(ia)[:, 0:16], [Bpf[ia]], [Bsm])
                        TT("dve", ax, bank(ia)[:, 16:32], xr, ALU.subtract, [Bpf[ia], Bsm], [Bsm])
                        ACT(ex, ax, AF.Exp, [Bsm], [Bsm])
                        TT("dve", dtds, dt_, ex, ALU.mult, [Bsm], [Bsm])
                        TT("dve", v3(xdt[:]), v3(xs_tok[:]), b16(dt_), ALU.mult, [Bxs, Bsm], [Bxdt])
                        TT("pool", v3(xdd[:]), v3(xs_tok[:]), b16(dtds), ALU.mult, [Bxs, Bsm], [Bxdd])
                        i = rotf.get(2)
                        for nb in range(2):
                            for k in range(8):
                                MM(bank(i + nb), uT[:, k, tk], w_in_sb[:, k, 2576 + nb * 512:2576 + (nb + 1) * 512],
                                   k == 0, k == 7, [Bwin_v, BuT], [Bpf[i + nb]], inc=(k == 7))
                        ACT(vt[:, cur, :], bank(i, 2), AF.Identity, [Bpf[i], Bpf[i + 1]], [Bvt[cur]])
                        for g in range(2):
                            TT("dve" if g == 0 else "pool", Mm[:, g * 8:(g + 1) * 8, :], E[:, g * 8:(g + 1) * 8, :],
                               CBm[:, g, :].unsqueeze(1).to_broadcast([128, 8, 128]), ALU.mult, [BE, BCBm], [BM])
                        for half in range(2):
                            ip = rotf.get()
                            for q in range(4):
                                cbk = half * 4 + q
                                g = cbk // 2
                                if c == 0:
                                    MM(bank(ip)[:, q * 128:(q + 1) * 128], vt[:, cur, cbk * 128:(cbk + 1) * 128],
                                       pmat_b[:, 8 + g, :], True, True, [Bvt[cur], Bc], [Bpf[ip]], inc=(q == 3))
                                else:
                                    MM(bank(ip)[:, q * 128:(q + 1) * 128], vt[:, cur, cbk * 128:(cbk + 1) * 128],
                                       pmat_b[:, g, :], True, False, [Bvt[cur], Bc], [Bpf[ip]])
                                    MM(bank(ip)[:, q * 128:(q + 1) * 128], vt[:, prv, cbk * 128:(cbk + 1) * 128],
                                       pmat_b[:, 4 + g, :], False, True, [Bvt[prv], Bc], [Bpf[ip]], inc=(q == 3))
                            ACT(dTs[:, half * 4:(half + 1) * 4, :].rearrange("p q t -> p (q t)"), bank(ip), AF.Identity,
                                [Bpf[ip]], [BdT])
                        ipp = rotf.get(2)
                        for oc in range(8):
                            g = oc // 2
                            dst = bank(ipp + oc // 4)[:, (oc % 4) * 128:(oc % 4 + 1) * 128]
                            for cc2 in range(2):
                                MM(dst, pool_sb[:, g * 2 + cc2, (oc % 2) * 128:(oc % 2 + 1) * 128], dTs[:, g * 2 + cc2, :],
                                   cc2 == 0, cc2 == 1, [Bpool, BdT], [Bpf[ipp + oc // 4]], inc=(cc2 == 1 and oc % 4 == 3))
                        for hb in range(2):
                            TT("pool" if False else "dve", catT[:, cbf, 8 + hb * 4:8 + (hb + 1) * 4, :],
                               bank(ipp + hb).rearrange("p (q t) -> p q t", t=128),
                               psT[:, hb * 4:(hb + 1) * 4].unsqueeze(2).to_broadcast([128, 4, 128]), ALU.mult,
                               [Bpf[ipp + hb], Bpv], [Bcat[cbf]])
                        io = rotf.get(2)
                        for g in range(2):
                            MM(bank(io + g), xbcT[:, 10 + g, tk], S_bf[:, g * 512:(g + 1) * 512], True, True,
                               [BxT, BSb], [Bpf[io + g]], inc=True)
                        TT("dve", v3(yb[:]), bank(io, 2).rearrange("p (h j) -> p h j", j=64), b16(ea), ALU.mult,
                           [Bpf[io], Bpf[io + 1], Bsm], [Byb])
                        ist = rotf.get(2)
                        for g in range(2):
                            MM(bank(ist + g), B_tok[:, g * 128:(g + 1) * 128], xdd[:, g * 512:(g + 1) * 512], True, True,
                               [BBt, Bxdd], [Bpf[ist + g]], inc=True)
                        iy = rotf.get(2)
                        for nb in range(2):
                            MM(bank(iy + nb), ident_b[:], xsd[:, nb * 512:(nb + 1) * 512], True, False,
                               [Bc, Bxsd], [Bpf[iy + nb]])
                            for hh in range(8):
                                h = nb * 8 + hh
                                MM(bank(iy + nb)[:, hh * 64:(hh + 1) * 64], Mm[:, h, :], xdt[:, h * 64:(h + 1) * 64],
                                   False, hh == 7, [BM, Bxdt], [Bpf[iy + nb]], inc=(hh == 7))
                        TT("dve", S[:], S[:], bank(ist, 2), ALU.add, [BS, Bpf[ist], Bpf[ist + 1]], [BS])
                        CP("pool", S_bf[:], S[:], [BS], [BSb])
                        TT("dve", yb[:], yb[:], bank(iy, 2), ALU.add, [Byb, Bpf[iy], Bpf[iy + 1]], [Byb])
                        TT("dve", yb[:], yb[:], zs[:], ALU.mult, [Byb, Bzs], [Byb])
                        if j < 3:
                            step1(j + 1)
                        flush_pending()
                        r2, Br2 = rstd_of(yb[:], Byb)
                        TS("dve", ygn[:], yb[:], r2, ALU.mult, [Byb, Br2], [Bygn])
                        transpose8(ygn, Bygn, ssdT, catT[:, cbf], Bcat[cbf], 0)
                        pend[0] = c
                flush_pending()

        def attention(l):
            with ExitStack() as ph:
                wq_sb = sb(ph, "wq_sb", [128, 8, 1024], BF16)
                wo_sb = sb(ph, "wo_sb", [128, 8, 1024], BF16)
                wkv_sb = sb(ph, "wkv_sb", [128, 8, 1024], BF16)
                Bwq, Bwo, Bwkv = Buf("wq"), Buf("wo"), Buf("wkv")
                memT = sb(ph, "memT", [128, 8, 256], BF16)
                BmT = Buf("memT")
                kT = sb(ph, "kT", [128, 8, 256], BF16)
                BkT = Buf("kT")
                Vs = sb(ph, "Vs", [128, 2, 1024], BF16)
                BV = Buf("Vs")
                qT = sb(ph, "qT", [128, 8, 512], BF16)
                BqT = Buf("qT")
                pT = sb(ph, "pT", [128, 8, 512], BF16)
                BpT = [Buf("pT%d" % i) for i in range(8)]
                rinv = sb(ph, "rinv", [128, 2, 512], F32)
                Bri = [Buf("rinv0"), Buf("rinv1")]
                oTn = sb(ph, "oTn", [128, 8, 512], BF16)
                BoT = Buf("oTn")
                mh = sb(ph, "mh", [128, 1024], F32)
                Bmh = Buf("mh")

                for mc in range(2):
                    fw.dma("sp", mh[:], mem[mc * 128:(mc + 1) * 128, :], writes=[Bmh], sembuf=Bmh)
                    norm_transpose(mh[:], Bmh, pv[:, l, PV_MEM:PV_MEM + 8], memT, BmT, mc * 128)
                for c in range(min(4, NCH)):
                    load_h(hbuf, c)
                for k in range(8):
                    fw.dma("pool", wkv_sb[:, k, :], w_kv[l, k * 128:(k + 1) * 128, 0:1024], writes=[Bwkv], sembuf=Bwkv)
                load_w(wq_sb, Bwq, w_q[l], 8)
                load_w(wo_sb, Bwo, w_o[l], 8)
                for oc in range(8):
                    i = rotf.get()
                    for k in range(8):
                        MM(bank(i)[:, 0:256], wkv_sb[:, k, oc * 128:(oc + 1) * 128], memT[:, k, :], k == 0, k == 7,
                           [Bwkv, BmT], [Bpf[i]], inc=(k == 7))
                    ACT(kT[:, oc, :], bank(i)[:, 0:256], AF.Identity, [Bpf[i]], [BkT])
                for k in range(8):
                    fw.dma("pool", wkv_sb[:, k, :], w_kv[l, k * 128:(k + 1) * 128, 1024:2048], reads=[], writes=[Bwkv], sembuf=Bwkv)
                for mc in range(2):
                    for nb in range(2):
                        i = rotf.get()
                        for k in range(8):
                            MM(bank(i), memT[:, k, mc * 128:(mc + 1) * 128], wkv_sb[:, k, nb * 512:(nb + 1) * 512],
                               k == 0, k == 7, [Bwkv, BmT], [Bpf[i]], inc=(k == 7))
                        ACT(Vs[:, mc, nb * 512:(nb + 1) * 512], bank(i), AF.Identity, [Bpf[i]], [BV])

                gT = pv[:, l, PV_XAT:PV_XAT + 8]
                for t in range(NT):
                    for j in range(4):
                        norm_transpose(hs[:, j, :], Bhs[j], gT, uT, BuT, j * 128)
                    for oc in range(8):
                        i = rotf.get()
                        for k in range(8):
                            MM(bank(i), wq_sb[:, k, oc * 128:(oc + 1) * 128], uT[:, k, :], k == 0, k == 7,
                               [Bwq, BuT], [Bpf[i]], inc=(k == 7))
                        if oc % 2 == 0:
                            ACT(qT[:, oc, :], bank(i), AF.Identity, [Bpf[i]], [BqT])
                        else:
                            CP("dve", qT[:, oc, :], bank(i), [Bpf[i]], [BqT])
                    for hd in range(4):
                        for mc in range(2):
                            i = rotf.get()
                            for dd in range(2):
                                dc = 2 * hd + dd
                                MM(bank(i), kT[:, dc, mc * 128:(mc + 1) * 128], qT[:, dc, :], dd == 0, dd == 1,
                                   [BkT, BqT], [Bpf[i]], inc=(dd == 1))
                            p = hd * 2 + mc
                            ACT(pT[:, p, :], bank(i), AF.Exp, [Bpf[i]], [BpT[p]], scale=1.0 / 16.0)
                    for hd in range(4):
                        pi = [hd * 2, hd * 2 + 1]
                        i = rotf.get()
                        for mc in range(2):
                            MM(bank(i), ones_b[:], pT[:, pi[mc], :], mc == 0, mc == 1, [Bc, BpT[pi[mc]]], [Bpf[i]], inc=(mc == 1))
                        rr = hd % 2
                        ACT(rinv[:, rr, :], bank(i), AF.Ln, [Bpf[i]], [Bri[rr]])
                        ACT(rinv[:, rr, :], rinv[:, rr, :], AF.Exp, [Bri[rr]], [Bri[rr]], scale=-1.0)
                        for dd in range(2):
                            dc = 2 * hd + dd
                            i2 = rotf.get()
                            for mc in range(2):
                                MM(bank(i2), Vs[:, mc, dc * 128:(dc + 1) * 128], pT[:, pi[mc], :], mc == 0, mc == 1,
                                   [BV, BpT[pi[mc]]], [Bpf[i2]], inc=(mc == 1))
                            TT("dve", oTn[:, dc, :], bank(i2), rinv[:, rr, :], ALU.mult, [Bpf[i2], Bri[rr]], [BoT])
                    for j in range(4):
                        c = t * 4 + j
                        iz = rotf.get(2)
                        for nb in range(2):
                            for k in range(8):
                                MM(bank(iz + nb), oTn[:, k, j * 128:(j + 1) * 128], wo_sb[:, k, nb * 512:(nb + 1) * 512],
                                   k == 0, k == 7, [BoT, Bwo], [Bpf[iz + nb]], inc=(k == 7))
                        TT("dve", hs[:, j, :], hs[:, j, :], bank(iz, 2), ALU.add, [Bhs[j], Bpf[iz], Bpf[iz + 1]], [Bhs[j]])
                        store_h(hbuf, c)
                        if c + 4 < NCH:
                            load_h(hbuf, c + 4)

        def ffn(l, last):
            with ExitStack() as ph:
                wgu_sb = sb(ph, "wgu_sb", [128, 8, 5632], BF16)
                wd_sb = sb(ph, "wd_sb", [128, 22, 1024], BF16)
                Bwd = Buf("wd")
                FB = [(0, 6), (6, 12), (12, 17), (17, 22)]
                Bwgu_b = [Buf("wgu%d" % i) for i in range(4)]
                Bwgu_f = {}
                for bi, (f0, f1) in enumerate(FB):
                    for f in range(f0, f1):
                        Bwgu_f[f] = Bwgu_b[bi]
                hT = sb(ph, "hT", [128, 22, 512], BF16)
                BhT = Buf("hT")
                sg = sb(ph, "sg", [128, 2, 512], F32)
                Bsg = [Buf("sg0"), Buf("sg1")]
                ob = sb(ph, "ob", [128, 1024], F32)
                Bob = Buf("ob")
                for c in range(min(4, NCH)):
                    load_h(hbuf, c)
                for k in range(8):
                    fw.dma("pool", wgu_sb[:, k, :], w_gu[l, k * 128:(k + 1) * 128, :], writes=Bwgu_b, sembuf=Bwgu_b[0])
                load_w(wd_sb, Bwd, w_dn[l], 22)
                gT = pv[:, l, PV_FFN:PV_FFN + 8]
                for t in range(NT):
                    for j in range(4):
                        norm_transpose(hs[:, j, :], Bhs[j], gT, uT, BuT, j * 128)
                    for f in range(22):
                        ig = rotf.get()
                        for k in range(8):
                            MM(bank(ig), wgu_sb[:, k, f * 128:(f + 1) * 128], uT[:, k, :], k == 0, k == 7,
                               [Bwgu_f[f], BuT], [Bpf[ig]], inc=(k == 7))
                        iu = rotf.get()
                        for k in range(8):
                            MM(bank(iu), wgu_sb[:, k, 2816 + f * 128:2816 + (f + 1) * 128], uT[:, k, :], k == 0, k == 7,
                               [Bwgu_f[f], BuT], [Bpf[iu]], inc=(k == 7))
                        r = f % 2
                        ACT(sg[:, r, :], bank(ig), AF.Silu, [Bpf[ig]], [Bsg[r]])
                        TT("dve", hT[:, f, :], sg[:, r, :], bank(iu), ALU.mult, [Bsg[r], Bpf[iu]], [BhT])
                    for j in range(4):
                        c = t * 4 + j
                        iz = rotf.get(2)
                        for nb in range(2):
                            for f in range(22):
                                MM(bank(iz + nb), hT[:, f, j * 128:(j + 1) * 128], wd_sb[:, f, nb * 512:(nb + 1) * 512],
                                   f == 0, f == 21, [BhT, Bwd], [Bpf[iz + nb]], inc=(f == 21))
                        TT("dve", hs[:, j, :], hs[:, j, :], bank(iz, 2), ALU.add, [Bhs[j], Bpf[iz], Bpf[iz + 1]], [Bhs[j]])
                        if last:
                            r, Br = rstd_of(hs[:, j, :], Bhs[j])
                            STT("dve", ob[:], hs[:, j, :], r, fn_b[:], ALU.mult, ALU.mult, [Bhs[j], Br, Bcf], [Bob])
                            store_h(out, c, ob[:], Bob)
                        else:
                            store_h(hbuf, c)
                        if c + 4 < NCH:
                            load_h(hbuf, c + 4)

        stages = []
        for l in range(depth):
            stages += [("mix", l), ("att", l), ("ffn", l)]
        for si, (kind, l) in enumerate(stages):
            is_last = (si == len(stages) - 1) or (stop_after == si)
            if si > 0:
                fw.barrier()
            if kind == "mix":
                mixer(l, x if l == 0 else hbuf)
            elif kind == "att":
                attention(l)
            else:
                ffn(l, last=(si == len(stages) - 1))
            if stop_after == si and si != len(stages) - 1:
                for c in range(NCH):
                    load_h(hbuf, c)
                    store_h(out, c)
                break
        fw.finish(Bh + Bhs)
        build.stats = (fw.nops, fw.nwaits, fw.nsem)
    return nc


def host_consts():
    idn = np.eye(128, dtype=np.float32)
    k = np.arange(128)[:, None]
    s = np.arange(128)[None, :]
    U = (k > s).astype(np.float32)
    tri = (k <= s).astype(np.float32)
    ones = np.ones((128, 128), np.float32)
    mats = [idn, U, tri, ones]
    P, Pp, P0 = [], [], []
    sidx = np.arange(128)[:, None]
    tidx = np.arange(128)[None, :]
    for w in (2, 4, 8, 16):
        p = ((sidx <= tidx) & (sidx > tidx - w)).astype(np.float32) / w - (sidx == tidx)
        pp = ((sidx - 128) > (tidx - w)).astype(np.float32) / w
        cnt = np.minimum(tidx + 1, w).astype(np.float32)
        p0 = ((sidx <= tidx) & (sidx > tidx - w)).astype(np.float32) / cnt - (sidx == tidx)
        P.append(p.astype(np.float32)); Pp.append(pp.astype(np.float32)); P0.append(p0.astype(np.float32))
    mats += P + Pp + P0
    mats.append(np.zeros((128, 128), np.float32))
    return np.stack(mats).astype(np.float32)


def host_pvec(inp, depth):
    def T8(v):
        return np.ascontiguousarray(v.reshape(8, 128).T)
    pv = np.zeros((depth, 128, NPV), np.float32)
    for l in range(depth):
        pv[l, :, 0:8] = T8(inp["mix_norm"][l])
        pv[l, :, 8:16] = T8(inp["xattn_norm"][l])
        pv[l, :, 16:24] = T8(inp["ffn_norm"][l])
        pv[l, :, 24:32] = T8(inp["mem_norm"][l])
        pv[l, :, 32:40] = T8(inp["ssd_norm"][l])
        cw = inp["conv_w"][l]
        pv[l, :, 40:88] = cw.reshape(4, 12, 128).transpose(2, 1, 0).reshape(128, 48)
        pv[l, :, 88:100] = inp["conv_b"][l].reshape(12, 128).T
        pv[l, :, 100:108] = T8(inp["pool_scale"][l])
    rowp = np.stack([np.stack([inp["dt_bias"][l], inp["a_log"][l], inp["d_skip"][l]]) for l in range(depth)])
    return pv, np.ascontiguousarray(rowp.astype(np.float32))


_CACHE = {}


def run(inputs, ntok, depth, ncores, stop_after=None):
    inp = {k: np.asarray(v) for k, v in inputs.items()}
    key = (ntok, depth, stop_after)
    if key not in _CACHE:
        _CACHE[key] = build(ntok, depth, stop_after)
    nc = _CACHE[key]
    pv, rowp = host_pvec(inp, depth)
    cm = host_consts()
    in_maps = []
    for c in range(ncores):
        b = c % inp["x"].shape[0]
        m = {
            "x": np.ascontiguousarray(inp["x"][b, :ntok]),
            "mem": np.ascontiguousarray(inp["mem"][b]),
            "pvec": pv, "rowp": rowp, "cmat": cm,
            "final_norm": np.ascontiguousarray(inp["final_norm"]),
        }
        for k in ("w_in", "pool_w", "w_out_mix", "w_q", "w_kv", "w_o", "w_gate_up", "w_down"):
            m[k] = np.ascontiguousarray(inp[k][:depth])
        in_maps.append(m)
    res = run_bass_kernel_spmd(nc, in_maps, core_ids=list(range(ncores)))
    return [r["out"] for r in res.results]


def kernel(**inputs):
    B = np.asarray(inputs["x"]).shape[0]
    outs = run(inputs, SEQ, DEPTH, B)
    return np.stack(outs[:B]).astype(np.float32)
```

```python
import numpy as np
import concourse.bass as bass
import concourse.mybir as mybir
from concourse.bass_utils import run_bass_kernel_spmd
from contextlib import ExitStack

F32 = mybir.dt.float32
BF16 = mybir.dt.bfloat16
AF = mybir.ActivationFunctionType
ALU = mybir.AluOpType

DEPTH = 2
SEQ = 8192
EPS = 1e-6
NPV = 116
import os
DBG = int(os.environ.get('KDBG', '99'))


class Buf:
    __slots__ = ("name", "last_w", "readers", "sem", "semcnt")

    def __init__(self, name):
        self.name = name
        self.last_w = None
        self.readers = []
        self.sem = None
        self.semcnt = 0


class FW:
    def __init__(self, nc, es):
        self.nc = nc
        self.es = es
        self.engs = {"pe": nc.tensor, "dve": nc.vector, "act": nc.scalar,
                     "pool": nc.gpsimd, "sp": nc.sync}
        self.sems = {}
        self.cnt = {}
        for k in self.engs:
            self.sems[k] = es.enter_context(nc.semaphore("s_" + k))
            self.cnt[k] = 0
        self.seen = {k: {} for k in self.engs}
        self.nsem = len(self.engs)
        self.nwaits = 0
        self.nops = 0
        self.dmabufs = []

    def _wait(self, eng, deps):
        e = self.engs[eng]
        best = {}
        for (k, v) in deps:
            if k == eng and eng == "pe":
                continue
            if best.get(k, 0) < v:
                best[k] = v
        for k, v in best.items():
            if self.seen[eng].get(k, 0) < v:
                e.wait_ge(self.sems[k], v)
                self.seen[eng][k] = v
                self.nwaits += 1

    def _deps(self, reads, writes):
        deps = []
        for b in reads:
            if b.last_w is not None:
                deps.append(b.last_w)
        for b in writes:
            if b.last_w is not None:
                deps.append(b.last_w)
            deps.extend(b.readers)
        return deps

    def _record(self, ident, reads, writes):
        for b in reads:
            b.readers.append(ident)
            if len(b.readers) > 48:
                best = {}
                for k, v in b.readers:
                    if best.get(k, 0) < v:
                        best[k] = v
                b.readers = list(best.items())
        for b in writes:
            b.last_w = ident
            b.readers = []

    def op(self, eng, fn, reads=(), writes=(), inc=True):
        self._wait(eng, self._deps(reads, writes))
        inst = fn()
        self.nops += 1
        if inc:
            self.cnt[eng] += 1
            inst.then_inc(self.sems[eng], 1)
            ident = (eng, self.cnt[eng])
        else:
            ident = (eng, self.cnt[eng] + 1)
        self._record(ident, reads, writes)
        return inst

    def dma(self, eng, out, in_, reads=(), writes=(), sembuf=None, **kw):
        if sembuf.sem is None:
            key = "d%d" % self.nsem
            sembuf.sem = key
            self.dmabufs.append(sembuf)
            self.sems[key] = self.es.enter_context(self.nc.semaphore(key))
            self.nsem += 1
        self._wait(eng, self._deps(reads, writes))
        inst = self.engs[eng].dma_start(out=out, in_=in_, **kw)
        sembuf.semcnt += 16
        inst.then_inc(self.sems[sembuf.sem], 16)
        ident = (sembuf.sem, sembuf.semcnt)
        self._record(ident, reads, writes)
        return inst

    def barrier(self):
        deps = [(k, self.cnt[k]) for k in self.engs if self.cnt[k] > 0]
        deps += [(b.sem, b.semcnt) for b in self.dmabufs]
        for eng in self.engs:
            self._wait(eng, deps)

    def finish(self, bufs):
        deps = []
        for b in bufs:
            if b.last_w is not None:
                deps.append(b.last_w)
            deps.extend(b.readers)
        self._wait("sp", deps)


class Rot:
    def __init__(self, n):
        self.n = n
        self.p = 0

    def get(self, k=1):
        if self.p + k > self.n:
            self.p = 0
        r = self.p
        self.p += k
        if self.p >= self.n:
            self.p = 0
        return r


def build(ntok, depth, stop_after=None):
    nc = bass.Bass("TRN2", target_bir_lowering=False)
    NCH = ntok // 128
    NT = ntok // 512

    def din(name, shape):
        return nc.dram_tensor(name, shape, F32, kind="ExternalInput").ap()

    x = din("x", [ntok, 1024])
    mem = din("mem", [256, 1024])
    w_in = din("w_in", [depth, 1024, 3600])
    pool_w = din("pool_w", [depth, 4, 256, 256])
    w_out = din("w_out_mix", [depth, 2048, 1024])
    w_q = din("w_q", [depth, 1024, 1024])
    w_kv = din("w_kv", [depth, 1024, 2048])
    w_o = din("w_o", [depth, 1024, 1024])
    w_gu = din("w_gate_up", [depth, 1024, 5632])
    w_dn = din("w_down", [depth, 2816, 1024])
    pvec = din("pvec", [depth, 128, NPV])
    rowp = din("rowp", [depth, 3, 16])
    fnorm = din("final_norm", [1024])
    cmat = din("cmat", [17, 128, 128])
    out = nc.dram_tensor("out", [ntok, 1024], F32, kind="ExternalOutput").ap()
    hbuf = nc.dram_tensor("hbuf", [ntok, 1024], F32).ap()

    with ExitStack() as es:
        fw = FW(nc, es)

        uniq = [0]

        def sb(st, name, shape, dt):
            uniq[0] += 1
            return st.enter_context(nc.sbuf_tensor("%s_%d" % (name, uniq[0]), shape, dt))

        def MM(out_, lhsT, rhs, st, sp, R, W, inc=False):
            fw.op("pe", lambda: nc.tensor.matmul(out_, lhsT, rhs, start=st, stop=sp), R, W, inc=inc)

        def TR(out_, in_, idn, R, W, inc=False):
            fw.op("pe", lambda: nc.tensor.transpose(out_, in_, idn), R, W, inc=inc)

        def ACT(out_, in_, func, R, W, **kw):
            fw.op("act", lambda: nc.scalar.activation(out=out_, in_=in_, func=func, **kw), R, W)

        def TT(eng, out_, in0, in1, op, R, W):
            e = fw.engs[eng]
            fw.op(eng, lambda: e.tensor_tensor(out=out_, in0=in0, in1=in1, op=op), R, W)

        def TS(eng, out_, in0, s1, op0, R, W):
            e = fw.engs[eng]
            fw.op(eng, lambda: e.tensor_scalar(out=out_, in0=in0, scalar1=s1, scalar2=None, op0=op0), R, W)

        def STT(eng, out_, in0, scalar, in1, op0, op1, R, W):
            e = fw.engs[eng]
            fw.op(eng, lambda: e.scalar_tensor_tensor(out=out_, in0=in0, scalar=scalar, in1=in1, op0=op0, op1=op1), R, W)

        def CP(eng, out_, in_, R, W):
            e = fw.engs[eng]
            fw.op(eng, lambda: e.tensor_copy(out=out_, in_=in_), R, W)

        def MS(eng, ap, val, W):
            e = fw.engs[eng]
            fw.op(eng, lambda: e.memset(ap, val), (), W)

        pf = es.enter_context(nc.psum_tensor("pf", [128, 6 * 512], F32))
        pb = es.enter_context(nc.psum_tensor("pb", [128, 2 * 1024], BF16))
        Bpf = [Buf("pf%d" % i) for i in range(6)]
        Bpb = [Buf("pb%d" % i) for i in range(2)]
        rotf = Rot(6)
        rotb = Rot(2)

        def bank(i, n=1):
            return pf[:, i * 512:(i + n) * 512]

        def bbank(i):
            return pb[:, i * 1024:(i + 1) * 1024]

        cst = es
        ident_b = sb(cst, "ident_b", [128, 128], BF16)
        ones_b = sb(cst, "ones_b", [128, 128], BF16)
        pmat_b = sb(cst, "pmat_b", [128, 12, 128], BF16)
        cf = sb(cst, "cf", [128, 3, 128], F32)
        fn_b = sb(cst, "fn_b", [128, 1024], F32)
        Bc = Buf("consts")
        fw.dma("pool", ident_b[:], cmat[0], writes=[Bc], sembuf=Bc)
        fw.dma("pool", ones_b[:], cmat[3], writes=[Bc], sembuf=Bc)
        for i in range(12):
            fw.dma("pool", pmat_b[:, i, :], cmat[4 + i], writes=[Bc], sembuf=Bc)
        Bcf = Buf("constsf")
        for i in range(3):
            fw.dma("sp", cf[:, i, :], cmat[1 + i], writes=[Bcf], sembuf=Bcf)
        fw.dma("sp", fn_b[:], fnorm.partition_broadcast(128), writes=[Bcf], sembuf=Bcf)
        Umat = cf[:, 0, :]
        tri = cf[:, 1, :]
        onesf = cf[:, 2, :]

        pv = sb(cst, "pv", [128, depth, NPV], F32)
        rp = sb(cst, "rp", [128, depth, 48], F32)
        Bpv = Buf("pv")
        for l in range(depth):
            fw.dma("sp", pv[:, l, :], pvec[l], writes=[Bpv], sembuf=Bpv)
            for i in range(3):
                fw.dma("sp", rp[:, l, i * 16:(i + 1) * 16], rowp[l, i].partition_broadcast(128), writes=[Bpv], sembuf=Bpv)
            ACT(rp[:, l, 16:32], rp[:, l, 16:32], AF.Exp, [Bpv], [Bpv])
            TS("dve", rp[:, l, 16:32], rp[:, l, 16:32], -1.0, ALU.mult, [Bpv], [Bpv])
        PV_MIX, PV_XAT, PV_FFN, PV_MEM, PV_SSD, PV_CW, PV_CB, PV_PS = 0, 8, 16, 24, 32, 40, 88, 100

        hs = sb(cst, "hs", [128, 4, 1024], F32)
        Bhs = [Buf("hs%d" % i) for i in range(4)]
        Bh = [Buf("h%d" % c) for c in range(NCH)]
        ub2 = sb(cst, "ub", [128, 2, 1024], BF16)
        Bub2 = [Buf("ub0"), Buf("ub1")]
        rotu = Rot(2)
        junk = sb(cst, "junk", [128, 1024], BF16)
        Bjunk = Buf("junk")
        ssr = sb(cst, "ssr", [128, 4, 4], F32)
        Bss = [Buf("ss%d" % i) for i in range(4)]
        rots = Rot(4)
        uT = sb(cst, "uT", [128, 8, 512], BF16)
        BuT = Buf("uT")

        def rstd_of(src, Bsrc):
            i = rots.get()
            s = ssr[:, i, :]
            MS("pool", s[:, 0:1], 0.0, [Bss[i]])
            ACT(junk[:], src, AF.Square, [Bsrc, Bss[i]], [Bjunk, Bss[i]], accum_out=s[:, 0:1])
            ACT(s[:, 1:2], s[:, 0:1], AF.Ln, [Bss[i]], [Bss[i]], scale=1.0 / 1024, bias=EPS)
            ACT(s[:, 2:3], s[:, 1:2], AF.Exp, [Bss[i]], [Bss[i]], scale=-0.5)
            return s[:, 2:3], Bss[i]

        def norm_transpose(src, Bsrc, gT, dst, Bdst, tokoff):
            r, Br = rstd_of(src, Bsrc)
            u = rotu.get()
            ACT(ub2[:, u, :], src, AF.Identity, [Bsrc, Br], [Bub2[u]], scale=r)
            transpose8(ub2[:, u, :], Bub2[u], gT, dst, Bdst, tokoff)

        def transpose8(srcb, Bsrcb, gT, dst, Bdst, tokoff):
            i = rotb.get()
            for k in range(8):
                TR(bbank(i)[:, k * 128:(k + 1) * 128], srcb[:, k * 128:(k + 1) * 128], ident_b[:],
                   [Bsrcb, Bc], [Bpb[i]], inc=(k == 7))
            TT("dve", dst[:, 0:8, tokoff:tokoff + 128],
               bbank(i).rearrange("p (q t) -> p q t", t=128),
               gT.unsqueeze(2).to_broadcast([128, 8, 128]),
               ALU.mult, [Bpb[i], Bpv], [Bdst])

        def load_h(src, c):
            j = c % 4
            fw.dma("sp", hs[:, j, :], src[c * 128:(c + 1) * 128, :], reads=[Bh[c]], writes=[Bhs[j]], sembuf=Bhs[j])

        def store_h(dst, c, srcap=None, Bsrc=None):
            j = c % 4
            if srcap is None:
                srcap, Bsrc = hs[:, j, :], Bhs[j]
            fw.dma("sp", dst[c * 128:(c + 1) * 128, :], srcap, reads=[Bsrc], writes=[Bh[c]], sembuf=Bsrc)

        def load_w_cols(dst3, src2, nk, blocks):
            for Bb, ranges in blocks:
                for k in range(nk):
                    for (c0, c1) in ranges:
                        fw.dma("pool", dst3[:, k, c0:c1], src2[k * 128:(k + 1) * 128, c0:c1], writes=[Bb], sembuf=Bb)

        def load_w(dst3, Bdst, src2, nk):
            for k in range(nk):
                if os.environ.get('KNOW'):
                    continue
                fw.dma("pool", dst3[:, k, :], src2[k * 128:(k + 1) * 128, :], writes=[Bdst], sembuf=Bdst)

        def mixer(l, src):
            with ExitStack() as ph:
                w_in_sb = sb(ph, "w_in_sb", [128, 8, 3600], BF16)
                w_out_sb = sb(ph, "w_out_sb", [128, 16, 1024], BF16)
                pool_sb = sb(ph, "pool_sb", [128, 8, 256], BF16)
                Bwout, Bpool = Buf("w_out"), Buf("poolw")
                Bwin_x, Bwin_z, Bwin_v = Buf("w_in_x"), Buf("w_in_z"), Buf("w_in_v")
                for c in range(min(4, NCH)):
                    load_h(src, c)
                for k in range(8):
                    fw.dma("pool", w_in_sb[:, k, :], w_in[l, k * 128:(k + 1) * 128, :],
                           writes=[Bwin_x, Bwin_z, Bwin_v], sembuf=Bwin_x)
                for g in range(4):
                    for cc in range(2):
                        fw.dma("pool", pool_sb[:, g * 2 + cc, :], pool_w[l, g, cc * 128:(cc + 1) * 128, :],
                               writes=[Bpool], sembuf=Bpool)
                load_w(w_out_sb, Bwout, w_out[l], 16)

                raw = sb(ph, "raw", [128, 2, 516], F32)
                Braw = [Buf("raw0"), Buf("raw1")]
                acc = sb(ph, "acc", [128, 2, 512], F32)
                Bacc = [Buf("acc0"), Buf("acc1")]
                halo = sb(ph, "halo", [128, 12, 3], F32)
                Bhalo = Buf("halo")
                xbcT = sb(ph, "xbcT", [128, 12, 512], BF16)
                BxT = Buf("xbcT")
                zs = sb(ph, "zs", [128, 1024], BF16)
                Bzs = Buf("zs")
                vt = sb(ph, "vt", [128, 2, 1024], BF16)
                Bvt = [Buf("vt0"), Buf("vt1")]
                xs_tok = sb(ph, "xs_tok", [128, 1024], BF16)
                Bxs = Buf("xs_tok")
                B_tok = sb(ph, "B_tok", [128, 256], BF16)
                BBt = Buf("B_tok")
                sm = sb(ph, "sm", [128, 8, 16], F32)
                Bsm = Buf("sm")
                Wm = sb(ph, "Wm", [128, 16, 128], F32)
                BWm = Buf("Wm")
                E = sb(ph, "E", [128, 16, 128], BF16)
                BE = Buf("E")
                Mm = sb(ph, "Mm", [128, 16, 128], BF16)
                BM = Buf("Mm")
                CBm = sb(ph, "CBm", [128, 2, 128], F32)
                BCBm = Buf("CBm")
                xdt = sb(ph, "xdt", [128, 1024], BF16)
                Bxdt = Buf("xdt")
                xdd = sb(ph, "xdd", [128, 1024], BF16)
                Bxdd = Buf("xdd")
                xsd = sb(ph, "xsd", [128, 1024], BF16)
                Bxsd = Buf("xsd")
                S = sb(ph, "S", [128, 1024], F32)
                BS = Buf("S")
                S_bf = sb(ph, "S_bf", [128, 1024], BF16)
                BSb = Buf("S_bf")
                yb = sb(ph, "yb", [128, 1024], F32)
                Byb = Buf("yb")
                ygn = sb(ph, "ygn", [128, 1024], BF16)
                Bygn = Buf("ygn")
                catT = sb(ph, "catT", [128, 2, 16, 128], BF16)
                Bcat = [Buf("catT0"), Buf("catT1")]
                dTs = sb(ph, "dTs", [128, 8, 128], BF16)
                BdT = Buf("dTs")

                gT = pv[:, l, PV_MIX:PV_MIX + 8]
                ssdT = pv[:, l, PV_SSD:PV_SSD + 8]
                cw = pv[:, l, PV_CW:PV_CW + 48]
                cb = pv[:, l, PV_CB:PV_CB + 12]
                psT = pv[:, l, PV_PS:PV_PS + 8]
                dtb = rp[:, l, 0:16]
                a_b = rp[:, l, 16:32]
                dsk = rp[:, l, 32:48]

                MS("pool", halo[:], 0.0, [Bhalo])
                MS("pool", S[:], 0.0, [BS])
                MS("pool", S_bf[:], 0.0, [BSb])

                def b16(ap):
                    return ap.unsqueeze(2).to_broadcast([128, 16, 64])

                def v3(ap):
                    return ap.rearrange("p (h j) -> p h j", j=64)

                def step1(j):
                    tk = slice(j * 128, (j + 1) * 128)
                    xr, ax, ex, l1, dt_, adt, dtds, ea = [sm[:, q, :] for q in range(8)]
                    i = rotf.get()
                    for k in range(8):
                        MM(bank(i)[:, 0:16], uT[:, k, tk], w_in_sb[:, k, 2560:2576],
                           k == 0, k == 7, [Bwin_v, BuT], [Bpf[i]], inc=(k == 7))
                    TT("dve", xr, bank(i)[:, 0:16], dtb, ALU.add, [Bpf[i], Bpv], [Bsm])
                    ACT(ax, xr, AF.Abs, [Bsm], [Bsm])
                    ACT(ex, ax, AF.Exp, [Bsm], [Bsm], scale=-1.0)
                    ACT(l1, ex, AF.Ln, [Bsm], [Bsm], bias=1.0)
                    STT("dve", dt_, xr, 0.0, l1, ALU.max, ALU.add, [Bsm], [Bsm])
                    TT("dve", adt, dt_, a_b, ALU.mult, [Bsm, Bpv], [Bsm])
                    TT("pool", Wm[:], tri.unsqueeze(1).to_broadcast([128, 16, 128]),
                       adt.unsqueeze(2).to_broadcast([128, 16, 128]), ALU.mult, [Bcf, Bsm], [BWm])

                pend = [None]

                def outproj(c):
                    j = c % 4
                    cbf = c % 2
                    iz = rotf.get(2)
                    for nb in range(2):
                        for k in range(16):
                            MM(bank(iz + nb), catT[:, cbf, k, :], w_out_sb[:, k, nb * 512:(nb + 1) * 512],
                               k == 0, k == 15, [Bcat[cbf], Bwout], [Bpf[iz + nb]], inc=(k == 15))
                    TT("dve", hs[:, j, :], hs[:, j, :], bank(iz, 2), ALU.add, [Bhs[j], Bpf[iz], Bpf[iz + 1]], [Bhs[j]])
                    store_h(hbuf, c)
                    if c + 4 < NCH:
                        load_h(src, c + 4)

                def flush_pending():
                    if pend[0] is not None:
                        outproj(pend[0])
                        pend[0] = None

                for t in range(NT):
                    flush_pending()
                    for j in range(4):
                        norm_transpose(hs[:, j, :], Bhs[j], gT, uT, BuT, j * 128)
                    for cc in range(12):
                        i = rotf.get()
                        for k in range(8):
                            MM(bank(i), w_in_sb[:, k, 1024 + cc * 128:1024 + (cc + 1) * 128], uT[:, k, :],
                               k == 0, k == 7, [Bwin_x, BuT], [Bpf[i]], inc=(k == 7))
                        r = cc % 2
                        CP("pool", raw[:, r, 0:3], halo[:, cc, :], [Bhalo], [Braw[r]])
                        ACT(raw[:, r, 3:515], bank(i), AF.Identity, [Bpf[i]], [Braw[r]])
                        CP("pool", halo[:, cc, :], raw[:, r, 512:515], [Braw[r]], [Bhalo])
                        ACT(acc[:, r, :], bank(i), AF.Identity, [Bpf[i], Bpv], [Bacc[r]], scale=cw[:, cc * 4 + 3:cc * 4 + 4])
                        for k in range(3):
                            STT("dve", acc[:, r, :], raw[:, r, k:k + 512], cw[:, cc * 4 + k:cc * 4 + k + 1],
                                acc[:, r, :], ALU.mult, ALU.add, [Braw[r], Bpv, Bacc[r]], [Bacc[r]])
                        ACT(xbcT[:, cc, :], acc[:, r, :], AF.Silu, [Bacc[r], Bpv], [BxT], bias=cb[:, cc:cc + 1])
                    for j in range(4):
                        c = t * 4 + j
                        tk = slice(j * 128, (j + 1) * 128)
                        cur, prv = c % 2, (c + 1) % 2
                        cbf = c % 2
                        xr, ax, ex, l1, dt_, adt, dtds, ea = [sm[:, q, :] for q in range(8)]
                        if j == 0:
                            step1(j)

                        def step_z():
                            i = rotf.get(2)
                            for nb in range(2):
                                for k in range(8):
                                    MM(bank(i + nb), uT[:, k, tk], w_in_sb[:, k, nb * 512:(nb + 1) * 512],
                                       k == 0, k == 7, [Bwin_z, BuT], [Bpf[i + nb]], inc=(k == 7))
                            ACT(zs[:], bank(i, 2), AF.Silu, [Bpf[i], Bpf[i + 1]], [Bzs])

                        def step_v():
                            i = rotf.get(2)
                            for nb in range(2):
                                for k in range(8):
                                    MM(bank(i + nb), uT[:, k, tk], w_in_sb[:, k, 2576 + nb * 512:2576 + (nb + 1) * 512],
                                       k == 0, k == 7, [Bwin_v, BuT], [Bpf[i + nb]], inc=(k == 7))
                            ACT(vt[:, cur, :], bank(i, 2), AF.Identity, [Bpf[i], Bpf[i + 1]], [Bvt[cur]])

                        def step_pool():
                            for half in range(2):
                                ip = rotf.get()
                                for q in range(4):
                                    cbk = half * 4 + q
                                    g = cbk // 2
                                    if c == 0:
                                        MM(bank(ip)[:, q * 128:(q + 1) * 128], vt[:, cur, cbk * 128:(cbk + 1) * 128],
                                           pmat_b[:, 8 + g, :], True, True, [Bvt[cur], Bc], [Bpf[ip]], inc=(q == 3))
                                    else:
                                        MM(bank(ip)[:, q * 128:(q + 1) * 128], vt[:, cur, cbk * 128:(cbk + 1) * 128],
                                           pmat_b[:, g, :], True, False, [Bvt[cur], Bc], [Bpf[ip]])
                                        MM(bank(ip)[:, q * 128:(q + 1) * 128], vt[:, prv, cbk * 128:(cbk + 1) * 128],
                                           pmat_b[:, 4 + g, :], False, True, [Bvt[prv], Bc], [Bpf[ip]], inc=(q == 3))
                                ACT(dTs[:, half * 4:(half + 1) * 4, :].rearrange("p q t -> p (q t)"), bank(ip), AF.Identity,
                                    [Bpf[ip]], [BdT])
                            ipp = rotf.get(2)
                            for oc in range(8):
                                g = oc // 2
                                dst = bank(ipp + oc // 4)[:, (oc % 4) * 128:(oc % 4 + 1) * 128]
                                for cc2 in range(2):
                                    MM(dst, pool_sb[:, g * 2 + cc2, (oc % 2) * 128:(oc % 2 + 1) * 128], dTs[:, g * 2 + cc2, :],
                                       cc2 == 0, cc2 == 1, [Bpool, BdT], [Bpf[ipp + oc // 4]], inc=(cc2 == 1 and oc % 4 == 3))
                            for hb in range(2):
                                TT("pool" if False else "dve", catT[:, cbf, 8 + hb * 4:8 + (hb + 1) * 4, :],
                                   bank(ipp + hb).rearrange("p (q t) -> p q t", t=128),
                                   psT[:, hb * 4:(hb + 1) * 4].unsqueeze(2).to_broadcast([128, 4, 128]), ALU.mult,
                                   [Bpf[ipp + hb], Bpv], [Bcat[cbf]])

                        if j == 0:
                            step_z()
                            step_v()
                            step_pool()
                        ib = rotb.get()
                        for k in range(8):
                            TR(bbank(ib)[:, k * 128:(k + 1) * 128], xbcT[:, k, tk], ident_b[:],
                               [BxT, Bc], [Bpb[ib]], inc=(k == 7))
                        ACT(xs_tok[:], bbank(ib), AF.Identity, [Bpb[ib]], [Bxs])
                        ib = rotb.get()
                        for q in range(2):
                            TR(bbank(ib)[:, q * 128:(q + 1) * 128], xbcT[:, 8 + q, tk], ident_b[:],
                               [BxT, Bc], [Bpb[ib]], inc=(q == 1))
                        CP("dve", B_tok[:], bbank(ib)[:, 0:256], [Bpb[ib]], [BBt])
                        TT("pool", v3(xsd[:]), v3(xs_tok[:]), b16(dsk), ALU.mult, [Bxs, Bpv], [Bxsd])
                        if j > 0:
                            step_z()
                        ic = rotf.get()
                        for g in range(2):
                            MM(bank(ic)[:, g * 128:(g + 1) * 128], xbcT[:, 8 + g, tk], xbcT[:, 10 + g, tk],
                               True, True, [BxT], [Bpf[ic]], inc=(g == 1))
                        TT("dve", CBm[:], bank(ic)[:, 0:256].rearrange("p (g l) -> p g l", g=2),
                           tri.unsqueeze(1).to_broadcast([128, 2, 128]), ALU.mult, [Bpf[ic], Bcf], [BCBm])
                        i4 = rotf.get(4)
                        for q in range(4):
                            MM(bank(i4 + q), Umat, Wm[:, q * 4:(q + 1) * 4, :].rearrange("p h l -> p (h l)"),
                               True, True, [Bcf, BWm], [Bpf[i4 + q]], inc=True)
                        ia = rotf.get()
                        MM(bank(ia)[:, 0:16], tri, adt, True, True, [Bcf, Bsm], [Bpf[ia]])
                        MM(bank(ia)[:, 16:32], onesf, adt, True, True, [Bcf, Bsm], [Bpf[ia]], inc=True)
                        for q in range(4):
                            ACT(E[:, q * 4:(q + 1) * 4, :].rearrange("p h l -> p (h l)"), bank(i4 + q), AF.Exp,
                                [Bpf[i4 + q]], [BE])
                        cdb = l1
                        ACT(ea, bank(ia)[:, 0:16], AF.Exp, [Bpf[ia]], [Bsm])
                        ACT(cdb, bank(ia)[:, 16:32], AF.Exp, [Bpf[ia]], [Bsm])
                        TT("pool", v3(S[:]), v3(S[:]), b16(cdb), ALU.mult, [BS, Bsm], [BS])
                        CP("dve", xr, bank(ia)[:, 0:16], [Bpf[ia]], [Bsm])
                        TT("dve", ax, bank(ia)[:, 16:32], xr, ALU.subtract, [Bpf[ia], Bsm], [Bsm])
                        ACT(ex, ax, AF.Exp, [Bsm], [Bsm])
                        TT("dve", dtds, dt_, ex, ALU.mult, [Bsm], [Bsm])
                        TT("dve", v3(xdt[:]), v3(xs_tok[:]), b16(dt_), ALU.mult, [Bxs, Bsm], [Bxdt])
                        TT("pool", v3(xdd[:]), v3(xs_tok[:]), b16(dtds), ALU.mult, [Bxs, Bsm], [Bxdd])
                        if j > 0:
                            step_v()
                        for g in range(2):
                            TT("dve" if g == 0 else "pool", Mm[:, g * 8:(g + 1) * 8, :], E[:, g * 8:(g + 1) * 8, :],
                               CBm[:, g, :].unsqueeze(1).to_broadcast([128, 8, 128]), ALU.mult, [BE, BCBm], [BM])
                        if j > 0:
                            step_pool()
                        io = rotf.get(2)
                        for g in range(2):
                            MM(bank(io + g), xbcT[:, 10 + g, tk], S_bf[:, g * 512:(g + 1) * 512], True, True,
                               [BxT, BSb], [Bpf[io + g]], inc=True)
                        TT("dve", v3(yb[:]), bank(io, 2).rearrange("p (h j) -> p h j", j=64), b16(ea), ALU.mult,
                           [Bpf[io], Bpf[io + 1], Bsm], [Byb])
                        ist = rotf.get(2)
                        for g in range(2):
                            MM(bank(ist + g), B_tok[:, g * 128:(g + 1) * 128], xdd[:, g * 512:(g + 1) * 512], True, True,
                               [BBt, Bxdd], [Bpf[ist + g]], inc=True)
                        iy = rotf.get(2)
                        for nb in range(2):
                            MM(bank(iy + nb), ident_b[:], xsd[:, nb * 512:(nb + 1) * 512], True, False,
                               [Bc, Bxsd], [Bpf[iy + nb]])
                            for hh in range(8):
                                h = nb * 8 + hh
                                MM(bank(iy + nb)[:, hh * 64:(hh + 1) * 64], Mm[:, h, :], xdt[:, h * 64:(h + 1) * 64],
                                   False, hh == 7, [BM, Bxdt], [Bpf[iy + nb]], inc=(hh == 7))
                        TT("dve", S[:], S[:], bank(ist, 2), ALU.add, [BS, Bpf[ist], Bpf[ist + 1]], [BS])
                        CP("pool", S_bf[:], S[:], [BS], [BSb])
                        TT("dve", yb[:], yb[:], bank(iy, 2), ALU.add, [Byb, Bpf[iy], Bpf[iy + 1]], [Byb])
                        TT("dve", yb[:], yb[:], zs[:], ALU.mult, [Byb, Bzs], [Byb])
                        if j < 3:
                            step1(j + 1)
                        flush_pending()
                        r2, Br2 = rstd_of(yb[:], Byb)
                        TS("dve", ygn[:], yb[:], r2, ALU.mult, [Byb, Br2], [Bygn])
                        transpose8(ygn, Bygn, ssdT, catT[:, cbf], Bcat[cbf], 0)
                        pend[0] = c
                flush_pending()

        def attention(l):
            with ExitStack() as ph:
                wq_sb = sb(ph, "wq_sb", [128, 8, 1024], BF16)
                wo_sb = sb(ph, "wo_sb", [128, 8, 1024], BF16)
                wkv_sb = sb(ph, "wkv_sb", [128, 8, 1024], BF16)
                Bwq, Bwo, Bwkv = Buf("wq"), Buf("wo"), Buf("wkv")
                memT = sb(ph, "memT", [128, 8, 256], BF16)
                BmT = Buf("memT")
                kT = sb(ph, "kT", [128, 8, 256], BF16)
                BkT = Buf("kT")
                Vs = sb(ph, "Vs", [128, 2, 1024], BF16)
                BV = Buf("Vs")
                qT = sb(ph, "qT", [128, 8, 512], BF16)
                BqT = Buf("qT")
                pT = sb(ph, "pT", [128, 8, 512], BF16)
                BpT = [Buf("pT%d" % i) for i in range(8)]
                rinv = sb(ph, "rinv", [128, 2, 512], F32)
                Bri = [Buf("rinv0"), Buf("rinv1")]
                oTn = sb(ph, "oTn", [128, 8, 512], BF16)
                BoT = Buf("oTn")
                mh = sb(ph, "mh", [128, 1024], F32)
                Bmh = Buf("mh")

                for mc in range(2):
                    fw.dma("sp", mh[:], mem[mc * 128:(mc + 1) * 128, :], writes=[Bmh], sembuf=Bmh)
                    norm_transpose(mh[:], Bmh, pv[:, l, PV_MEM:PV_MEM + 8], memT, BmT, mc * 128)
                for c in range(min(4, NCH)):
                    load_h(hbuf, c)
                for k in range(8):
                    fw.dma("pool", wkv_sb[:, k, :], w_kv[l, k * 128:(k + 1) * 128, 0:1024], writes=[Bwkv], sembuf=Bwkv)
                load_w(wq_sb, Bwq, w_q[l], 8)
                load_w(wo_sb, Bwo, w_o[l], 8)
                for oc in range(8):
                    i = rotf.get()
                    for k in range(8):
                        MM(bank(i)[:, 0:256], wkv_sb[:, k, oc * 128:(oc + 1) * 128], memT[:, k, :], k == 0, k == 7,
                           [Bwkv, BmT], [Bpf[i]], inc=(k == 7))
                    ACT(kT[:, oc, :], bank(i)[:, 0:256], AF.Identity, [Bpf[i]], [BkT])
                for k in range(8):
                    fw.dma("pool", wkv_sb[:, k, :], w_kv[l, k * 128:(k + 1) * 128, 1024:2048], reads=[], writes=[Bwkv], sembuf=Bwkv)
                for mc in range(2):
                    for nb in range(2):
                        i = rotf.get()
                        for k in range(8):
                            MM(bank(i), memT[:, k, mc * 128:(mc + 1) * 128], wkv_sb[:, k, nb * 512:(nb + 1) * 512],
                               k == 0, k == 7, [Bwkv, BmT], [Bpf[i]], inc=(k == 7))
                        ACT(Vs[:, mc, nb * 512:(nb + 1) * 512], bank(i), AF.Identity, [Bpf[i]], [BV])

                gT = pv[:, l, PV_XAT:PV_XAT + 8]
                for t in range(NT):
                    for j in range(4):
                        norm_transpose(hs[:, j, :], Bhs[j], gT, uT, BuT, j * 128)
                    for oc in range(8):
                        i = rotf.get()
                        for k in range(8):
                            MM(bank(i), wq_sb[:, k, oc * 128:(oc + 1) * 128], uT[:, k, :], k == 0, k == 7,
                               [Bwq, BuT], [Bpf[i]], inc=(k == 7))
                        if oc % 2 == 0:
                            ACT(qT[:, oc, :], bank(i), AF.Identity, [Bpf[i]], [BqT])
                        else:
                            CP("dve", qT[:, oc, :], bank(i), [Bpf[i]], [BqT])
                    for hd in range(4):
                        for mc in range(2):
                            i = rotf.get()
                            for dd in range(2):
                                dc = 2 * hd + dd
                                MM(bank(i), kT[:, dc, mc * 128:(mc + 1) * 128], qT[:, dc, :], dd == 0, dd == 1,
                                   [BkT, BqT], [Bpf[i]], inc=(dd == 1))
                            p = hd * 2 + mc
                            ACT(pT[:, p, :], bank(i), AF.Exp, [Bpf[i]], [BpT[p]], scale=1.0 / 16.0)
                    for hd in range(4):
                        pi = [hd * 2, hd * 2 + 1]
                        i = rotf.get()
                        for mc in range(2):
                            MM(bank(i), ones_b[:], pT[:, pi[mc], :], mc == 0, mc == 1, [Bc, BpT[pi[mc]]], [Bpf[i]], inc=(mc == 1))
                        rr = hd % 2
                        ACT(rinv[:, rr, :], bank(i), AF.Ln, [Bpf[i]], [Bri[rr]])
                        ACT(rinv[:, rr, :], rinv[:, rr, :], AF.Exp, [Bri[rr]], [Bri[rr]], scale=-1.0)
                        for dd in range(2):
                            dc = 2 * hd + dd
                            i2 = rotf.get()
                            for mc in range(2):
                                MM(bank(i2), Vs[:, mc, dc * 128:(dc + 1) * 128], pT[:, pi[mc], :], mc == 0, mc == 1,
                                   [BV, BpT[pi[mc]]], [Bpf[i2]], inc=(mc == 1))
                            TT("dve", oTn[:, dc, :], bank(i2), rinv[:, rr, :], ALU.mult, [Bpf[i2], Bri[rr]], [BoT])
                    for j in range(4):
                        c = t * 4 + j
                        iz = rotf.get(2)
                        for nb in range(2):
                            for k in range(8):
                                MM(bank(iz + nb), oTn[:, k, j * 128:(j + 1) * 128], wo_sb[:, k, nb * 512:(nb + 1) * 512],
                                   k == 0, k == 7, [BoT, Bwo], [Bpf[iz + nb]], inc=(k == 7))
                        TT("dve", hs[:, j, :], hs[:, j, :], bank(iz, 2), ALU.add, [Bhs[j], Bpf[iz], Bpf[iz + 1]], [Bhs[j]])
                        store_h(hbuf, c)
                        if c + 4 < NCH:
                            load_h(hbuf, c + 4)

        def ffn(l, last):
            with ExitStack() as ph:
                wgu_sb = sb(ph, "wgu_sb", [128, 8, 5632], BF16)
                wd_sb = sb(ph, "wd_sb", [128, 22, 1024], BF16)
                Bwd = Buf("wd")
                FB = [(0, 6), (6, 12), (12, 17), (17, 22)]
                Bwgu_b = [Buf("wgu%d" % i) for i in range(4)]
                Bwgu_f = {}
                for bi, (f0, f1) in enumerate(FB):
                    for f in range(f0, f1):
                        Bwgu_f[f] = Bwgu_b[bi]
                hT = sb(ph, "hT", [128, 22, 512], BF16)
                BhT = Buf("hT")
                sg = sb(ph, "sg", [128, 2, 512], F32)
                Bsg = [Buf("sg0"), Buf("sg1")]
                ob = sb(ph, "ob", [128, 1024], F32)
                Bob = Buf("ob")
                for c in range(min(4, NCH)):
                    load_h(hbuf, c)
                for k in range(8):
                    fw.dma("pool", wgu_sb[:, k, :], w_gu[l, k * 128:(k + 1) * 128, :], writes=Bwgu_b, sembuf=Bwgu_b[0])
                load_w(wd_sb, Bwd, w_dn[l], 22)
                gT = pv[:, l, PV_FFN:PV_FFN + 8]
                for t in range(NT):
                    for j in range(4):
                        norm_transpose(hs[:, j, :], Bhs[j], gT, uT, BuT, j * 128)
                    for f in range(22):
                        ig = rotf.get()
                        for k in range(8):
                            MM(bank(ig), wgu_sb[:, k, f * 128:(f + 1) * 128], uT[:, k, :], k == 0, k == 7,
                               [Bwgu_f[f], BuT], [Bpf[ig]], inc=(k == 7))
                        iu = rotf.get()
                        for k in range(8):
                            MM(bank(iu), wgu_sb[:, k, 2816 + f * 128:2816 + (f + 1) * 128], uT[:, k, :], k == 0, k == 7,
                               [Bwgu_f[f], BuT], [Bpf[iu]], inc=(k == 7))
                        r = f % 2
                        ACT(sg[:, r, :], bank(ig), AF.Silu, [Bpf[ig]], [Bsg[r]])
                        TT("dve", hT[:, f, :], sg[:, r, :], bank(iu), ALU.mult, [Bsg[r], Bpf[iu]], [BhT])
                    for j in range(4):
                        c = t * 4 + j
                        iz = rotf.get(2)
                        for nb in range(2):
                            for f in range(22):
                                MM(bank(iz + nb), hT[:, f, j * 128:(j + 1) * 128], wd_sb[:, f, nb * 512:(nb + 1) * 512],
                                   f == 0, f == 21, [BhT, Bwd], [Bpf[iz + nb]], inc=(f == 21))
                        TT("dve", hs[:, j, :], hs[:, j, :], bank(iz, 2), ALU.add, [Bhs[j], Bpf[iz], Bpf[iz + 1]], [Bhs[j]])
                        if last:
                            r, Br = rstd_of(hs[:, j, :], Bhs[j])
                            STT("dve", ob[:], hs[:, j, :], r, fn_b[:], ALU.mult, ALU.mult, [Bhs[j], Br, Bcf], [Bob])
                            store_h(out, c, ob[:], Bob)
                        else:
                            store_h(hbuf, c)
                        if c + 4 < NCH:
                            load_h(hbuf, c + 4)

        stages = []
        for l in range(depth):
            stages += [("mix", l), ("att", l), ("ffn", l)]
        for si, (kind, l) in enumerate(stages):
            is_last = (si == len(stages) - 1) or (stop_after == si)
            if si > 0:
                fw.barrier()
            if kind == "mix":
                mixer(l, x if l == 0 else hbuf)
            elif kind == "att":
                attention(l)
            else:
                ffn(l, last=(si == len(stages) - 1))
            if stop_after == si and si != len(stages) - 1:
                for c in range(NCH):
                    load_h(hbuf, c)
                    store_h(out, c)
                break
        fw.finish(Bh + Bhs)
        build.stats = (fw.nops, fw.nwaits, fw.nsem)
    return nc


def host_consts():
    idn = np.eye(128, dtype=np.float32)
    k = np.arange(128)[:, None]
    s = np.arange(128)[None, :]
    U = (k > s).astype(np.float32)
    tri = (k <= s).astype(np.float32)
    ones = np.ones((128, 128), np.float32)
    mats = [idn, U, tri, ones]
    P, Pp, P0 = [], [], []
    sidx = np.arange(128)[:, None]
    tidx = np.arange(128)[None, :]
    for w in (2, 4, 8, 16):
        p = ((sidx <= tidx) & (sidx > tidx - w)).astype(np.float32) / w - (sidx == tidx)
        pp = ((sidx - 128) > (tidx - w)).astype(np.float32) / w
        cnt = np.minimum(tidx + 1, w).astype(np.float32)
        p0 = ((sidx <= tidx) & (sidx > tidx - w)).astype(np.float32) / cnt - (sidx == tidx)
        P.append(p.astype(np.float32)); Pp.append(pp.astype(np.float32)); P0.append(p0.astype(np.float32))
    mats += P + Pp + P0
    mats.append(np.zeros((128, 128), np.float32))
    return np.stack(mats).astype(np.float32)


def host_pvec(inp, depth):
    def T8(v):
        return np.ascontiguousarray(v.reshape(8, 128).T)
    pv = np.zeros((depth, 128, NPV), np.float32)
    for l in range(depth):
        pv[l, :, 0:8] = T8(inp["mix_norm"][l])
        pv[l, :, 8:16] = T8(inp["xattn_norm"][l])
        pv[l, :, 16:24] = T8(inp["ffn_norm"][l])
        pv[l, :, 24:32] = T8(inp["mem_norm"][l])
        pv[l, :, 32:40] = T8(inp["ssd_norm"][l])
        cw = inp["conv_w"][l]
        pv[l, :, 40:88] = cw.reshape(4, 12, 128).transpose(2, 1, 0).reshape(128, 48)
        pv[l, :, 88:100] = inp["conv_b"][l].reshape(12, 128).T
        pv[l, :, 100:108] = T8(inp["pool_scale"][l])
    rowp = np.stack([np.stack([inp["dt_bias"][l], inp["a_log"][l], inp["d_skip"][l]]) for l in range(depth)])
    return pv, np.ascontiguousarray(rowp.astype(np.float32))


_CACHE = {}


def run(inputs, ntok, depth, ncores, stop_after=None):
    inp = {k: np.asarray(v) for k, v in inputs.items()}
    key = (ntok, depth, stop_after)
    if key not in _CACHE:
        _CACHE[key] = build(ntok, depth, stop_after)
    nc = _CACHE[key]
    pv, rowp = host_pvec(inp, depth)
    cm = host_consts()
    in_maps = []
    for c in range(ncores):
        b = c % inp["x"].shape[0]
        m = {
            "x": np.ascontiguousarray(inp["x"][b, :ntok]),
            "mem": np.ascontiguousarray(inp["mem"][b]),
            "pvec": pv, "rowp": rowp, "cmat": cm,
            "final_norm": np.ascontiguousarray(inp["final_norm"]),
        }
        for k in ("w_in", "pool_w", "w_out_mix", "w_q", "w_kv", "w_o", "w_gate_up", "w_down"):
            m[k] = np.ascontiguousarray(inp[k][:depth])
        in_maps.append(m)
    res = run_bass_kernel_spmd(nc, in_maps, core_ids=list(range(ncores)))
    return [r["out"] for r in res.results]


def kernel(**inputs):
    B = np.asarray(inputs["x"]).shape[0]
    outs = run(inputs, SEQ, DEPTH, B)
    return np.stack(outs[:B]).astype(np.float32)
```

```python
import numpy as np
import concourse.bass as bass
import concourse.mybir as mybir
from concourse.bass_utils import run_bass_kernel_spmd
from contextlib import ExitStack

F32 = mybir.dt.float32
BF16 = mybir.dt.bfloat16
AF = mybir.ActivationFunctionType
ALU = mybir.AluOpType

DEPTH = 2
SEQ = 8192
EPS = 1e-6
NPV = 116
import os
DBG = int(os.environ.get('KDBG', '99'))


class Buf:
    __slots__ = ("name", "last_w", "readers", "sem", "semcnt")

    def __init__(self, name):
        self.name = name
        self.last_w = None
        self.readers = []
        self.sem = None
        self.semcnt = 0


class FW:
    def __init__(self, nc, es):
        self.nc = nc
        self.es = es
        self.engs = {"pe": nc.tensor, "dve": nc.vector, "act": nc.scalar,
                     "pool": nc.gpsimd, "sp": nc.sync}
        self.sems = {}
        self.cnt = {}
        for k in self.engs:
            self.sems[k] = es.enter_context(nc.semaphore("s_" + k))
            self.cnt[k] = 0
        self.seen = {k: {} for k in self.engs}
        self.nsem = len(self.engs)
        self.nwaits = 0
        self.nops = 0
        self.dmabufs = []

    def _wait(self, eng, deps):
        e = self.engs[eng]
        best = {}
        for (k, v) in deps:
            if k == eng and eng == "pe":
                continue
            if best.get(k, 0) < v:
                best[k] = v
        for k, v in best.items():
            if self.seen[eng].get(k, 0) < v:
                e.wait_ge(self.sems[k], v)
                self.seen[eng][k] = v
                self.nwaits += 1

    def _deps(self, reads, writes):
        deps = []
        for b in reads:
            if b.last_w is not None:
                deps.append(b.last_w)
        for b in writes:
            if b.last_w is not None:
                deps.append(b.last_w)
            deps.extend(b.readers)
        return deps

    def _record(self, ident, reads, writes):
        for b in reads:
            b.readers.append(ident)
            if len(b.readers) > 48:
                best = {}
                for k, v in b.readers:
                    if best.get(k, 0) < v:
                        best[k] = v
                b.readers = list(best.items())
        for b in writes:
            b.last_w = ident
            b.readers = []

    def op(self, eng, fn, reads=(), writes=(), inc=True):
        self._wait(eng, self._deps(reads, writes))
        inst = fn()
        self.nops += 1
        if inc:
            self.cnt[eng] += 1
            inst.then_inc(self.sems[eng], 1)
            ident = (eng, self.cnt[eng])
        else:
            ident = (eng, self.cnt[eng] + 1)
        self._record(ident, reads, writes)
        return inst

    def dma(self, eng, out, in_, reads=(), writes=(), sembuf=None, **kw):
        if sembuf.sem is None:
            key = "d%d" % self.nsem
            sembuf.sem = key
            self.dmabufs.append(sembuf)
            self.sems[key] = self.es.enter_context(self.nc.semaphore(key))
            self.nsem += 1
        self._wait(eng, self._deps(reads, writes))
        inst = self.engs[eng].dma_start(out=out, in_=in_, **kw)
        sembuf.semcnt += 16
        inst.then_inc(self.sems[sembuf.sem], 16)
        ident = (sembuf.sem, sembuf.semcnt)
        self._record(ident, reads, writes)
        return inst

    def barrier(self):
        deps = [(k, self.cnt[k]) for k in self.engs if self.cnt[k] > 0]
        deps += [(b.sem, b.semcnt) for b in self.dmabufs]
        for eng in self.engs:
            self._wait(eng, deps)

    def finish(self, bufs):
        deps = []
        for b in bufs:
            if b.last_w is not None:
                deps.append(b.last_w)
            deps.extend(b.readers)
        self._wait("sp", deps)


class Rot:
    def __init__(self, n):
        self.n = n
        self.p = 0

    def get(self, k=1):
        if self.p + k > self.n:
            self.p = 0
        r = self.p
        self.p += k
        if self.p >= self.n:
            self.p = 0
        return r


def build(ntok, depth, stop_after=None):
    nc = bass.Bass("TRN2", target_bir_lowering=False)
    NCH = ntok // 128
    NT = ntok // 512

    def din(name, shape):
        return nc.dram_tensor(name, shape, F32, kind="ExternalInput").ap()

    x = din("x", [ntok, 1024])
    mem = din("mem", [256, 1024])
    w_in = din("w_in", [depth, 1024, 3600])
    pool_w = din("pool_w", [depth, 4, 256, 256])
    w_out = din("w_out_mix", [depth, 2048, 1024])
    w_q = din("w_q", [depth, 1024, 1024])
    w_kv = din("w_kv", [depth, 1024, 2048])
    w_o = din("w_o", [depth, 1024, 1024])
    w_gu = din("w_gate_up", [depth, 1024, 5632])
    w_dn = din("w_down", [depth, 2816, 1024])
    pvec = din("pvec", [depth, 128, NPV])
    rowp = din("rowp", [depth, 3, 16])
    fnorm = din("final_norm", [1024])
    cmat = din("cmat", [17, 128, 128])
    out = nc.dram_tensor("out", [ntok, 1024], F32, kind="ExternalOutput").ap()
    hbuf = nc.dram_tensor("hbuf", [ntok, 1024], F32).ap()

    with ExitStack() as es:
        fw = FW(nc, es)

        uniq = [0]

        def sb(st, name, shape, dt):
            uniq[0] += 1
            return st.enter_context(nc.sbuf_tensor("%s_%d" % (name, uniq[0]), shape, dt))

        def MM(out_, lhsT, rhs, st, sp, R, W, inc=False):
            fw.op("pe", lambda: nc.tensor.matmul(out_, lhsT, rhs, start=st, stop=sp), R, W, inc=inc)

        def TR(out_, in_, idn, R, W, inc=False):
            fw.op("pe", lambda: nc.tensor.transpose(out_, in_, idn), R, W, inc=inc)

        def ACT(out_, in_, func, R, W, **kw):
            fw.op("act", lambda: nc.scalar.activation(out=out_, in_=in_, func=func, **kw), R, W)

        def TT(eng, out_, in0, in1, op, R, W):
            e = fw.engs[eng]
            fw.op(eng, lambda: e.tensor_tensor(out=out_, in0=in0, in1=in1, op=op), R, W)

        def TS(eng, out_, in0, s1, op0, R, W):
            e = fw.engs[eng]
            fw.op(eng, lambda: e.tensor_scalar(out=out_, in0=in0, scalar1=s1, scalar2=None, op0=op0), R, W)

        def STT(eng, out_, in0, scalar, in1, op0, op1, R, W):
            e = fw.engs[eng]
            fw.op(eng, lambda: e.scalar_tensor_tensor(out=out_, in0=in0, scalar=scalar, in1=in1, op0=op0, op1=op1), R, W)

        def CP(eng, out_, in_, R, W):
            e = fw.engs[eng]
            fw.op(eng, lambda: e.tensor_copy(out=out_, in_=in_), R, W)

        def MS(eng, ap, val, W):
            e = fw.engs[eng]
            fw.op(eng, lambda: e.memset(ap, val), (), W)

        pf = es.enter_context(nc.psum_tensor("pf", [128, 6 * 512], F32))
        pb = es.enter_context(nc.psum_tensor("pb", [128, 2 * 1024], BF16))
        Bpf = [Buf("pf%d" % i) for i in range(6)]
        Bpb = [Buf("pb%d" % i) for i in range(2)]
        rotf = Rot(6)
        rotb = Rot(2)

        def bank(i, n=1):
            return pf[:, i * 512:(i + n) * 512]

        def bbank(i):
            return pb[:, i * 1024:(i + 1) * 1024]

        cst = es
        ident_b = sb(cst, "ident_b", [128, 128], BF16)
        ones_b = sb(cst, "ones_b", [128, 128], BF16)
        pmat_b = sb(cst, "pmat_b", [128, 12, 128], BF16)
        cf = sb(cst, "cf", [128, 3, 128], F32)
        fn_b = sb(cst, "fn_b", [128, 1024], F32)
        Bc = Buf("consts")
        fw.dma("pool", ident_b[:], cmat[0], writes=[Bc], sembuf=Bc)
        fw.dma("pool", ones_b[:], cmat[3], writes=[Bc], sembuf=Bc)
        for i in range(12):
            fw.dma("pool", pmat_b[:, i, :], cmat[4 + i], writes=[Bc], sembuf=Bc)
        Bcf = Buf("constsf")
        for i in range(3):
            fw.dma("sp", cf[:, i, :], cmat[1 + i], writes=[Bcf], sembuf=Bcf)
        fw.dma("sp", fn_b[:], fnorm.partition_broadcast(128), writes=[Bcf], sembuf=Bcf)
        Umat = cf[:, 0, :]
        tri = cf[:, 1, :]
        onesf = cf[:, 2, :]

        pv = sb(cst, "pv", [128, depth, NPV], F32)
        rp = sb(cst, "rp", [128, depth, 48], F32)
        Bpv = Buf("pv")
        for l in range(depth):
            fw.dma("sp", pv[:, l, :], pvec[l], writes=[Bpv], sembuf=Bpv)
            for i in range(3):
                fw.dma("sp", rp[:, l, i * 16:(i + 1) * 16], rowp[l, i].partition_broadcast(128), writes=[Bpv], sembuf=Bpv)
            ACT(rp[:, l, 16:32], rp[:, l, 16:32], AF.Exp, [Bpv], [Bpv])
            TS("dve", rp[:, l, 16:32], rp[:, l, 16:32], -1.0, ALU.mult, [Bpv], [Bpv])
        PV_MIX, PV_XAT, PV_FFN, PV_MEM, PV_SSD, PV_CW, PV_CB, PV_PS = 0, 8, 16, 24, 32, 40, 88, 100

        hs = sb(cst, "hs", [128, 4, 1024], F32)
        Bhs = [Buf("hs%d" % i) for i in range(4)]
        Bh = [Buf("h%d" % c) for c in range(NCH)]
        ub2 = sb(cst, "ub", [128, 2, 1024], BF16)
        Bub2 = [Buf("ub0"), Buf("ub1")]
        rotu = Rot(2)
        junk = sb(cst, "junk", [128, 1024], BF16)
        Bjunk = Buf("junk")
        ssr = sb(cst, "ssr", [128, 4, 4], F32)
        Bss = [Buf("ss%d" % i) for i in range(4)]
        rots = Rot(4)
        uT = sb(cst, "uT", [128, 8, 512], BF16)
        BuT = Buf("uT")

        def rstd_of(src, Bsrc):
            i = rots.get()
            s = ssr[:, i, :]
            MS("pool", s[:, 0:1], 0.0, [Bss[i]])
            ACT(junk[:], src, AF.Square, [Bsrc, Bss[i]], [Bjunk, Bss[i]], accum_out=s[:, 0:1])
            ACT(s[:, 1:2], s[:, 0:1], AF.Ln, [Bss[i]], [Bss[i]], scale=1.0 / 1024, bias=EPS)
            ACT(s[:, 2:3], s[:, 1:2], AF.Exp, [Bss[i]], [Bss[i]], scale=-0.5)
            return s[:, 2:3], Bss[i]

        def norm_transpose(src, Bsrc, gT, dst, Bdst, tokoff):
            r, Br = rstd_of(src, Bsrc)
            u = rotu.get()
            ACT(ub2[:, u, :], src, AF.Identity, [Bsrc, Br], [Bub2[u]], scale=r)
            transpose8(ub2[:, u, :], Bub2[u], gT, dst, Bdst, tokoff)

        def transpose8(srcb, Bsrcb, gT, dst, Bdst, tokoff):
            i = rotb.get()
            for k in range(8):
                TR(bbank(i)[:, k * 128:(k + 1) * 128], srcb[:, k * 128:(k + 1) * 128], ident_b[:],
                   [Bsrcb, Bc], [Bpb[i]], inc=(k == 7))
            TT("dve", dst[:, 0:8, tokoff:tokoff + 128],
               bbank(i).rearrange("p (q t) -> p q t", t=128),
               gT.unsqueeze(2).to_broadcast([128, 8, 128]),
               ALU.mult, [Bpb[i], Bpv], [Bdst])

        def load_h(src, c):
            j = c % 4
            fw.dma("sp", hs[:, j, :], src[c * 128:(c + 1) * 128, :], reads=[Bh[c]], writes=[Bhs[j]], sembuf=Bhs[j])

        def store_h(dst, c, srcap=None, Bsrc=None):
            j = c % 4
            if srcap is None:
                srcap, Bsrc = hs[:, j, :], Bhs[j]
            fw.dma("sp", dst[c * 128:(c + 1) * 128, :], srcap, reads=[Bsrc], writes=[Bh[c]], sembuf=Bsrc)

        def load_w_cols(dst3, src2, nk, blocks):
            for Bb, ranges in blocks:
                for k in range(nk):
                    for (c0, c1) in ranges:
                        fw.dma("pool", dst3[:, k, c0:c1], src2[k * 128:(k + 1) * 128, c0:c1], writes=[Bb], sembuf=Bb)

        def load_w(dst3, Bdst, src2, nk):
            for k in range(nk):
                if os.environ.get('KNOW'):
                    continue
                fw.dma("pool", dst3[:, k, :], src2[k * 128:(k + 1) * 128, :], writes=[Bdst], sembuf=Bdst)

        def mixer(l, src):
            with ExitStack() as ph:
                w_in_sb = sb(ph, "w_in_sb", [128, 8, 3600], BF16)
                w_out_sb = sb(ph, "w_out_sb", [128, 16, 1024], BF16)
                pool_sb = sb(ph, "pool_sb", [128, 8, 256], BF16)
                Bwout, Bpool = Buf("w_out"), Buf("poolw")
                Bwin_x, Bwin_z, Bwin_v = Buf("w_in_x"), Buf("w_in_z"), Buf("w_in_v")
                for c in range(min(4, NCH)):
                    load_h(src, c)
                for k in range(8):
                    fw.dma("pool", w_in_sb[:, k, :], w_in[l, k * 128:(k + 1) * 128, :],
                           writes=[Bwin_x, Bwin_z, Bwin_v], sembuf=Bwin_x)
                for g in range(4):
                    for cc in range(2):
                        fw.dma("pool", pool_sb[:, g * 2 + cc, :], pool_w[l, g, cc * 128:(cc + 1) * 128, :],
                               writes=[Bpool], sembuf=Bpool)
                load_w(w_out_sb, Bwout, w_out[l], 16)

                raw = sb(ph, "raw", [128, 2, 516], F32)
                Braw = [Buf("raw0"), Buf("raw1")]
                acc = sb(ph, "acc", [128, 2, 512], F32)
                Bacc = [Buf("acc0"), Buf("acc1")]
                halo = sb(ph, "halo", [128, 12, 3], F32)
                Bhalo = Buf("halo")
                xbcT = sb(ph, "xbcT", [128, 12, 512], BF16)
                BxT = Buf("xbcT")
                zs = sb(ph, "zs", [128, 1024], BF16)
                Bzs = Buf("zs")
                vt = sb(ph, "vt", [128, 2, 1024], BF16)
                Bvt = [Buf("vt0"), Buf("vt1")]
                xs_tok = sb(ph, "xs_tok", [128, 1024], BF16)
                Bxs = Buf("xs_tok")
                B_tok = sb(ph, "B_tok", [128, 256], BF16)
                BBt = Buf("B_tok")
                sm = sb(ph, "sm", [128, 8, 16], F32)
                Bsm = Buf("sm")
                Wm = sb(ph, "Wm", [128, 16, 128], F32)
                BWm = Buf("Wm")
                E = sb(ph, "E", [128, 16, 128], BF16)
                BE = Buf("E")
                Mm = sb(ph, "Mm", [128, 16, 128], BF16)
                BM = Buf("Mm")
                CBm = sb(ph, "CBm", [128, 2, 128], F32)
                BCBm = Buf("CBm")
                xdt = sb(ph, "xdt", [128, 1024], BF16)
                Bxdt = Buf("xdt")
                xdd = sb(ph, "xdd", [128, 1024], BF16)
                Bxdd = Buf("xdd")
                xsd = sb(ph, "xsd", [128, 1024], BF16)
                Bxsd = Buf("xsd")
                S = sb(ph, "S", [128, 1024], F32)
                BS = Buf("S")
                S_bf = sb(ph, "S_bf", [128, 1024], BF16)
                BSb = Buf("S_bf")
                yb = sb(ph, "yb", [128, 1024], F32)
                Byb = Buf("yb")
                ygn = sb(ph, "ygn", [128, 1024], BF16)
                Bygn = Buf("ygn")
                catT = sb(ph, "catT", [128, 2, 16, 128], BF16)
                Bcat = [Buf("catT0"), Buf("catT1")]
                dTs = sb(ph, "dTs", [128, 8, 128], BF16)
                BdT = Buf("dTs")

                gT = pv[:, l, PV_MIX:PV_MIX + 8]
                ssdT = pv[:, l, PV_SSD:PV_SSD + 8]
                cw = pv[:, l, PV_CW:PV_CW + 48]
                cb = pv[:, l, PV_CB:PV_CB + 12]
                psT = pv[:, l, PV_PS:PV_PS + 8]
                dtb = rp[:, l, 0:16]
                a_b = rp[:, l, 16:32]
                dsk = rp[:, l, 32:48]

                MS("pool", halo[:], 0.0, [Bhalo])
                MS("pool", S[:], 0.0, [BS])
                MS("pool", S_bf[:], 0.0, [BSb])

                def b16(ap):
                    return ap.unsqueeze(2).to_broadcast([128, 16, 64])

                def v3(ap):
                    return ap.rearrange("p (h j) -> p h j", j=64)

                def step1(j):
                    tk = slice(j * 128, (j + 1) * 128)
                    xr, ax, ex, l1, dt_, adt, dtds, ea = [sm[:, q, :] for q in range(8)]
                    i = rotf.get()
                    for k in range(8):
                        MM(bank(i)[:, 0:16], uT[:, k, tk], w_in_sb[:, k, 2560:2576],
                           k == 0, k == 7, [Bwin_v, BuT], [Bpf[i]], inc=(k == 7))
                    TT("dve", xr, bank(i)[:, 0:16], dtb, ALU.add, [Bpf[i], Bpv], [Bsm])
                    ACT(ax, xr, AF.Abs, [Bsm], [Bsm])
                    ACT(ex, ax, AF.Exp, [Bsm], [Bsm], scale=-1.0)
                    ACT(l1, ex, AF.Ln, [Bsm], [Bsm], bias=1.0)
                    STT("dve", dt_, xr, 0.0, l1, ALU.max, ALU.add, [Bsm], [Bsm])
                    TT("dve", adt, dt_, a_b, ALU.mult, [Bsm, Bpv], [Bsm])
                    TT("pool", Wm[:], tri.unsqueeze(1).to_broadcast([128, 16, 128]),
                       adt.unsqueeze(2).to_broadcast([128, 16, 128]), ALU.mult, [Bcf, Bsm], [BWm])

                pend = [None]

                def outproj(c):
                    j = c % 4
                    cbf = c % 2
                    iz = rotf.get(2)
                    for nb in range(2):
                        for k in range(16):
                            MM(bank(iz + nb), catT[:, cbf, k, :], w_out_sb[:, k, nb * 512:(nb + 1) * 512],
                               k == 0, k == 15, [Bcat[cbf], Bwout], [Bpf[iz + nb]], inc=(k == 15))
                    TT("dve", hs[:, j, :], hs[:, j, :], bank(iz, 2), ALU.add, [Bhs[j], Bpf[iz], Bpf[iz + 1]], [Bhs[j]])
                    store_h(hbuf, c)
                    if c + 4 < NCH:
                        load_h(src, c + 4)

                def flush_pending():
                    if pend[0] is not None:
                        outproj(pend[0])
                        pend[0] = None

                for t in range(NT):
                    flush_pending()
                    for j in range(4):
                        norm_transpose(hs[:, j, :], Bhs[j], gT, uT, BuT, j * 128)
                    for cc in range(12):
                        i = rotf.get()
                        for k in range(8):
                            MM(bank(i), w_in_sb[:, k, 1024 + cc * 128:1024 + (cc + 1) * 128], uT[:, k, :],
                               k == 0, k == 7, [Bwin_x, BuT], [Bpf[i]], inc=(k == 7))
                        r = cc % 2
                        CP("pool", raw[:, r, 0:3], halo[:, cc, :], [Bhalo], [Braw[r]])
                        ACT(raw[:, r, 3:515], bank(i), AF.Identity, [Bpf[i]], [Braw[r]])
                        CP("pool", halo[:, cc, :], raw[:, r, 512:515], [Braw[r]], [Bhalo])
                        ACT(acc[:, r, :], bank(i), AF.Identity, [Bpf[i], Bpv], [Bacc[r]], scale=cw[:, cc * 4 + 3:cc * 4 + 4])
                        for k in range(3):
                            STT("dve", acc[:, r, :], raw[:, r, k:k + 512], cw[:, cc * 4 + k:cc * 4 + k + 1],
                                acc[:, r, :], ALU.mult, ALU.add, [Braw[r], Bpv, Bacc[r]], [Bacc[r]])
                        ACT(xbcT[:, cc, :], acc[:, r, :], AF.Silu, [Bacc[r], Bpv], [BxT], bias=cb[:, cc:cc + 1])
                    for j in range(4):
                        c = t * 4 + j
                        tk = slice(j * 128, (j + 1) * 128)
                        cur, prv = c % 2, (c + 1) % 2
                        cbf = c % 2
                        xr, ax, ex, l1, dt_, adt, dtds, ea = [sm[:, q, :] for q in range(8)]
                        if j == 0:
                            step1(j)

                        def step_z():
                            i = rotf.get(2)
                            for nb in range(2):
                                for k in range(8):
                                    MM(bank(i + nb), uT[:, k, tk], w_in_sb[:, k, nb * 512:(nb + 1) * 512],
                                       k == 0, k == 7, [Bwin_z, BuT], [Bpf[i + nb]], inc=(k == 7))
                            ACT(zs[:], bank(i, 2), AF.Silu, [Bpf[i], Bpf[i + 1]], [Bzs])

                        def step_v():
                            i = rotf.get(2)
                            for nb in range(2):
                                for k in range(8):
                                    MM(bank(i + nb), uT[:, k, tk], w_in_sb[:, k, 2576 + nb * 512:2576 + (nb + 1) * 512],
                                       k == 0, k == 7, [Bwin_v, BuT], [Bpf[i + nb]], inc=(k == 7))
                            ACT(vt[:, cur, :], bank(i, 2), AF.Identity, [Bpf[i], Bpf[i + 1]], [Bvt[cur]])

                        def step_pool():
                            for half in range(2):
                                ip = rotf.get()
                                for q in range(4):
                                    cbk = half * 4 + q
                                    g = cbk // 2
                                    if c == 0:
                                        MM(bank(ip)[:, q * 128:(q + 1) * 128], vt[:, cur, cbk * 128:(cbk + 1) * 128],
                                           pmat_b[:, 8 + g, :], True, True, [Bvt[cur], Bc], [Bpf[ip]], inc=(q == 3))
                                    else:
                                        MM(bank(ip)[:, q * 128:(q + 1) * 128], vt[:, cur, cbk * 128:(cbk + 1) * 128],
                                           pmat_b[:, g, :], True, False, [Bvt[cur], Bc], [Bpf[ip]])
                                        MM(bank(ip)[:, q * 128:(q + 1) * 128], vt[:, prv, cbk * 128:(cbk + 1) * 128],
                                           pmat_b[:, 4 + g, :], False, True, [Bvt[prv], Bc], [Bpf[ip]], inc=(q == 3))
                                ACT(dTs[:, half * 4:(half + 1) * 4, :].rearrange("p q t -> p (q t)"), bank(ip), AF.Identity,
                                    [Bpf[ip]], [BdT])
                            ipp = rotf.get(2)
                            for oc in range(8):
                                g = oc // 2
                                dst = bank(ipp + oc // 4)[:, (oc % 4) * 128:(oc % 4 + 1) * 128]
                                for cc2 in range(2):
                                    MM(dst, pool_sb[:, g * 2 + cc2, (oc % 2) * 128:(oc % 2 + 1) * 128], dTs[:, g * 2 + cc2, :],
                                       cc2 == 0, cc2 == 1, [Bpool, BdT], [Bpf[ipp + oc // 4]], inc=(cc2 == 1 and oc % 4 == 3))
                            for hb in range(2):
                                TT("pool" if False else "dve", catT[:, cbf, 8 + hb * 4:8 + (hb + 1) * 4, :],
                                   bank(ipp + hb).rearrange("p (q t) -> p q t", t=128),
                                   psT[:, hb * 4:(hb + 1) * 4].unsqueeze(2).to_broadcast([128, 4, 128]), ALU.mult,
                                   [Bpf[ipp + hb], Bpv], [Bcat[cbf]])

                        if j == 0:
                            step_z()
                            step_v()
                            step_pool()
                        ib = rotb.get()
                        for k in range(8):
                            TR(bbank(ib)[:, k * 128:(k + 1) * 128], xbcT[:, k, tk], ident_b[:],
                               [BxT, Bc], [Bpb[ib]], inc=(k == 7))
                        ACT(xs_tok[:], bbank(ib), AF.Identity, [Bpb[ib]], [Bxs])
                        ib = rotb.get()
                        for q in range(2):
                            TR(bbank(ib)[:, q * 128:(q + 1) * 128], xbcT[:, 8 + q, tk], ident_b[:],
                               [BxT, Bc], [Bpb[ib]], inc=(q == 1))
                        CP("dve", B_tok[:], bbank(ib)[:, 0:256], [Bpb[ib]], [BBt])
                        TT("pool", v3(xsd[:]), v3(xs_tok[:]), b16(dsk), ALU.mult, [Bxs, Bpv], [Bxsd])
                        if j > 0:
                            step_z()
                        ic = rotf.get()
                        for g in range(2):
                            MM(bank(ic)[:, g * 128:(g + 1) * 128], xbcT[:, 8 + g, tk], xbcT[:, 10 + g, tk],
                               True, True, [BxT], [Bpf[ic]], inc=(g == 1))
                        TT("dve", CBm[:], bank(ic)[:, 0:256].rearrange("p (g l) -> p g l", g=2),
                           tri.unsqueeze(1).to_broadcast([128, 2, 128]), ALU.mult, [Bpf[ic], Bcf], [BCBm])
                        i4 = rotf.get(4)
                        for q in range(4):
                            MM(bank(i4 + q), Umat, Wm[:, q * 4:(q + 1) * 4, :].rearrange("p h l -> p (h l)"),
                               True, True, [Bcf, BWm], [Bpf[i4 + q]], inc=True)
                        ia = rotf.get()
                        MM(bank(ia)[:, 0:16], tri, adt, True, True, [Bcf, Bsm], [Bpf[ia]])
                        MM(bank(ia)[:, 16:32], onesf, adt, True, True, [Bcf, Bsm], [Bpf[ia]], inc=True)
                        for q in range(4):
                            ACT(E[:, q * 4:(q + 1) * 4, :].rearrange("p h l -> p (h l)"), bank(i4 + q), AF.Exp,
                                [Bpf[i4 + q]], [BE])
                        cdb = l1
                        ACT(ea, bank(ia)[:, 0:16], AF.Exp, [Bpf[ia]], [Bsm])
                        ACT(cdb, bank(ia)[:, 16:32], AF.Exp, [Bpf[ia]], [Bsm])
                        TT("pool", v3(S[:]), v3(S[:]), b16(cdb), ALU.mult, [BS, Bsm], [BS])
                        CP("dve", xr, bank(ia)[:, 0:16], [Bpf[ia]], [Bsm])
                        TT("dve", ax, bank(ia)[:, 16:32], xr, ALU.subtract, [Bpf[ia], Bsm], [Bsm])
                        ACT(ex, ax, AF.Exp, [Bsm], [Bsm])
                        TT("dve", dtds, dt_, ex, ALU.mult, [Bsm], [Bsm])
                        TT("dve", v3(xdt[:]), v3(xs_tok[:]), b16(dt_), ALU.mult, [Bxs, Bsm], [Bxdt])
                        TT("pool", v3(xdd[:]), v3(xs_tok[:]), b16(dtds), ALU.mult, [Bxs, Bsm], [Bxdd])
                        if j > 0:
                            step_v()
                        for g in range(2):
                            TT("dve" if g == 0 else "pool", Mm[:, g * 8:(g + 1) * 8, :], E[:, g * 8:(g + 1) * 8, :],
                               CBm[:, g, :].unsqueeze(1).to_broadcast([128, 8, 128]), ALU.mult, [BE, BCBm], [BM])
                        if j > 0:
                            step_pool()
                        io = rotf.get(2)
                        for g in range(2):
                            MM(bank(io + g), xbcT[:, 10 + g, tk], S_bf[:, g * 512:(g + 1) * 512], True, True,
                               [BxT, BSb], [Bpf[io + g]], inc=True)
                        TT("dve", v3(yb[:]), bank(io, 2).rearrange("p (h j) -> p h j", j=64), b16(ea), ALU.mult,
                           [Bpf[io], Bpf[io + 1], Bsm], [Byb])
                        ist = rotf.get(2)
                        for g in range(2):
                            MM(bank(ist + g), B_tok[:, g * 128:(g + 1) * 128], xdd[:, g * 512:(g + 1) * 512], True, True,
                               [BBt, Bxdd], [Bpf[ist + g]], inc=True)
                        iy = rotf.get(2)
                        for nb in range(2):
                            MM(bank(iy + nb), ident_b[:], xsd[:, nb * 512:(nb + 1) * 512], True, False,
                               [Bc, Bxsd], [Bpf[iy + nb]])
                            for hh in range(8):
                                h = nb * 8 + hh
                                MM(bank(iy + nb)[:, hh * 64:(hh + 1) * 64], Mm[:, h, :], xdt[:, h * 64:(h + 1) * 64],
                                   False, hh == 7, [BM, Bxdt], [Bpf[iy + nb]], inc=(hh == 7))
                        TT("dve", S[:], S[:], bank(ist, 2), ALU.add, [BS, Bpf[ist], Bpf[ist + 1]], [BS])
                        CP("pool", S_bf[:], S[:], [BS], [BSb])
                        TT("dve", yb[:], yb[:], bank(iy, 2), ALU.add, [Byb, Bpf[iy], Bpf[iy + 1]], [Byb])
                        TT("dve", yb[:], yb[:], zs[:], ALU.mult, [Byb, Bzs], [Byb])
                        if j < 3:
                            step1(j + 1)
                        flush_pending()
                        r2, Br2 = rstd_of(yb[:], Byb)
                        TS("dve", ygn[:], yb[:], r2, ALU.mult, [Byb, Br2], [Bygn])
                        transpose8(ygn, Bygn, ssdT, catT[:, cbf], Bcat[cbf], 0)
                        pend[0] = c
                flush_pending()

        def attention(l):
            with ExitStack() as ph:
                wq_sb = sb(ph, "wq_sb", [128, 8, 1024], BF16)
                wo_sb = sb(ph, "wo_sb", [128, 8, 1024], BF16)
                wkv_sb = sb(ph, "wkv_sb", [128, 8, 1024], BF16)
                Bwq, Bwo, Bwkv = Buf("wq"), Buf("wo"), Buf("wkv")
                memT = sb(ph, "memT", [128, 8, 256], BF16)
                BmT = Buf("memT")
                kT = sb(ph, "kT", [128, 8, 256], BF16)
                BkT = Buf("kT")
                Vs = sb(ph, "Vs", [128, 2, 1024], BF16)
                BV = Buf("Vs")
                qT = sb(ph, "qT", [128, 8, 512], BF16)
                BqT = Buf("qT")
                pT = sb(ph, "pT", [128, 8, 512], BF16)
                BpT = [Buf("pT%d" % i) for i in range(8)]
                rinv = sb(ph, "rinv", [128, 2, 512], F32)
                Bri = [Buf("rinv0"), Buf("rinv1")]
                oTn = sb(ph, "oTn", [128, 8, 512], BF16)
                BoT = Buf("oTn")
                mh = sb(ph, "mh", [128, 1024], F32)
                Bmh = Buf("mh")

                for mc in range(2):
                    fw.dma("sp", mh[:], mem[mc * 128:(mc + 1) * 128, :], writes=[Bmh], sembuf=Bmh)
                    norm_transpose(mh[:], Bmh, pv[:, l, PV_MEM:PV_MEM + 8], memT, BmT, mc * 128)
                for k in range(8):
                    fw.dma("pool", wkv_sb[:, k, :], w_kv[l, k * 128:(k + 1) * 128, 0:1024], writes=[Bwkv], sembuf=Bwkv)
                load_w(wq_sb, Bwq, w_q[l], 8)
                load_w(wo_sb, Bwo, w_o[l], 8)
                for oc in range(8):
                    i = rotf.get()
                    for k in range(8):
                        MM(bank(i)[:, 0:256], wkv_sb[:, k, oc * 128:(oc + 1) * 128], memT[:, k, :], k == 0, k == 7,
                           [Bwkv, BmT], [Bpf[i]], inc=(k == 7))
                    ACT(kT[:, oc, :], bank(i)[:, 0:256], AF.Identity, [Bpf[i]], [BkT])
                for k in range(8):
                    fw.dma("pool", wkv_sb[:, k, :], w_kv[l, k * 128:(k + 1) * 128, 1024:2048], reads=[], writes=[Bwkv], sembuf=Bwkv)
                for mc in range(2):
                    for nb in range(2):
                        i = rotf.get()
                        for k in range(8):
                            MM(bank(i), memT[:, k, mc * 128:(mc + 1) * 128], wkv_sb[:, k, nb * 512:(nb + 1) * 512],
                               k == 0, k == 7, [Bwkv, BmT], [Bpf[i]], inc=(k == 7))
                        ACT(Vs[:, mc, nb * 512:(nb + 1) * 512], bank(i), AF.Identity, [Bpf[i]], [BV])

                gT = pv[:, l, PV_XAT:PV_XAT + 8]
                hsA = sb(ph, "hsA", [128, 8, 1024], F32)
                BhsA = [Buf("hsA%d" % i) for i in range(8)]
                uTA = sb(ph, "uTA", [128, 2, 8, 512], BF16)
                BuTA = [Buf("uTA0"), Buf("uTA1")]
                ubA = sb(ph, "ubA", [128, 4, 1024], BF16)
                BubA = [Buf("ubA%d" % i) for i in range(4)]

                def loadA(c):
                    sl = c % 8
                    fw.dma("sp", hsA[:, sl, :], hbuf[c * 128:(c + 1) * 128, :], reads=[Bh[c]], writes=[BhsA[sl]],
                           sembuf=BhsA[sl])

                def storeA(c):
                    sl = c % 8
                    fw.dma("sp", hbuf[c * 128:(c + 1) * 128, :], hsA[:, sl, :], reads=[BhsA[sl]], writes=[Bh[c]],
                           sembuf=BhsA[sl])

                def normA(t):
                    for j in range(4):
                        c = t * 4 + j
                        r, Br = rstd_of(hsA[:, c % 8, :], BhsA[c % 8])
                        ACT(ubA[:, j, :], hsA[:, c % 8, :], AF.Identity, [BhsA[c % 8], Br], [BubA[j]], scale=r)

                def transA(t, j):
                    transpose8(ubA[:, j, :], BubA[j], gT, uTA[:, t % 2], BuTA[t % 2], j * 128)

                for c in range(min(8, NCH)):
                    loadA(c)
                normA(0)
                for j in range(4):
                    transA(0, j)
                for t in range(NT):
                    u = uTA[:, t % 2]
                    Bu = BuTA[t % 2]
                    for oc in range(8):
                        i = rotf.get()
                        for k in range(8):
                            MM(bank(i), wq_sb[:, k, oc * 128:(oc + 1) * 128], u[:, k, :], k == 0, k == 7,
                               [Bwq, Bu], [Bpf[i]], inc=(k == 7))
                        if oc % 2 == 0:
                            ACT(qT[:, oc, :], bank(i), AF.Identity, [Bpf[i]], [BqT])
                        else:
                            CP("dve", qT[:, oc, :], bank(i), [Bpf[i]], [BqT])
                    for hd in range(4):
                        for mc in range(2):
                            i = rotf.get()
                            for dd in range(2):
                                dc = 2 * hd + dd
                                MM(bank(i), kT[:, dc, mc * 128:(mc + 1) * 128], qT[:, dc, :], dd == 0, dd == 1,
                                   [BkT, BqT], [Bpf[i]], inc=(dd == 1))
                            p = hd * 2 + mc
                            ACT(pT[:, p, :], bank(i), AF.Exp, [Bpf[i]], [BpT[p]], scale=1.0 / 16.0)
                    if t + 1 < NT:
                        normA(t + 1)
                    for hd in range(4):
                        pi = [hd * 2, hd * 2 + 1]
                        i = rotf.get()
                        for mc in range(2):
                            MM(bank(i), ones_b[:], pT[:, pi[mc], :], mc == 0, mc == 1, [Bc, BpT[pi[mc]]], [Bpf[i]], inc=(mc == 1))
                        rr = hd % 2
                        ACT(rinv[:, rr, :], bank(i), AF.Ln, [Bpf[i]], [Bri[rr]])
                        ACT(rinv[:, rr, :], rinv[:, rr, :], AF.Exp, [Bri[rr]], [Bri[rr]], scale=-1.0)
                        for dd in range(2):
                            dc = 2 * hd + dd
                            i2 = rotf.get()
                            for mc in range(2):
                                MM(bank(i2), Vs[:, mc, dc * 128:(dc + 1) * 128], pT[:, pi[mc], :], mc == 0, mc == 1,
                                   [BV, BpT[pi[mc]]], [Bpf[i2]], inc=(mc == 1))
                            TT("dve", oTn[:, dc, :], bank(i2), rinv[:, rr, :], ALU.mult, [Bpf[i2], Bri[rr]], [BoT])
                    for j in range(4):
                        c = t * 4 + j
                        sl = c % 8
                        iz = rotf.get(2)
                        for nb in range(2):
                            for k in range(8):
                                MM(bank(iz + nb), oTn[:, k, j * 128:(j + 1) * 128], wo_sb[:, k, nb * 512:(nb + 1) * 512],
                                   k == 0, k == 7, [BoT, Bwo], [Bpf[iz + nb]], inc=(k == 7))
                        if t + 1 < NT:
                            transA(t + 1, j)
                        TT("dve", hsA[:, sl, :], hsA[:, sl, :], bank(iz, 2), ALU.add, [BhsA[sl], Bpf[iz], Bpf[iz + 1]], [BhsA[sl]])
                        storeA(c)
                        if c + 8 < NCH:
                            loadA(c + 8)

        def ffn(l, last):
            with ExitStack() as ph:
                wgu_sb = sb(ph, "wgu_sb", [128, 8, 5632], BF16)
                wd_sb = sb(ph, "wd_sb", [128, 22, 1024], BF16)
                Bwd = Buf("wd")
                FB = [(0, 6), (6, 12), (12, 17), (17, 22)]
                Bwgu_b = [Buf("wgu%d" % i) for i in range(4)]
                Bwgu_f = {}
                for bi, (f0, f1) in enumerate(FB):
                    for f in range(f0, f1):
                        Bwgu_f[f] = Bwgu_b[bi]
                hT = sb(ph, "hT", [128, 22, 512], BF16)
                BhT = Buf("hT")
                sg = sb(ph, "sg", [128, 2, 512], F32)
                Bsg = [Buf("sg0"), Buf("sg1")]
                ob = sb(ph, "ob", [128, 1024], F32)
                Bob = Buf("ob")
                for c in range(min(4, NCH)):
                    load_h(hbuf, c)
                for k in range(8):
                    fw.dma("pool", wgu_sb[:, k, :], w_gu[l, k * 128:(k + 1) * 128, :], writes=Bwgu_b, sembuf=Bwgu_b[0])
                load_w(wd_sb, Bwd, w_dn[l], 22)
                gT = pv[:, l, PV_FFN:PV_FFN + 8]
                for t in range(NT):
                    for j in range(4):
                        norm_transpose(hs[:, j, :], Bhs[j], gT, uT, BuT, j * 128)
                    for f in range(22):
                        ig = rotf.get()
                        for k in range(8):
                            MM(bank(ig), wgu_sb[:, k, f * 128:(f + 1) * 128], uT[:, k, :], k == 0, k == 7,
                               [Bwgu_f[f], BuT], [Bpf[ig]], inc=(k == 7))
                        iu = rotf.get()
                        for k in range(8):
                            MM(bank(iu), wgu_sb[:, k, 2816 + f * 128:2816 + (f + 1) * 128], uT[:, k, :], k == 0, k == 7,
                               [Bwgu_f[f], BuT], [Bpf[iu]], inc=(k == 7))
                        r = f % 2
                        ACT(sg[:, r, :], bank(ig), AF.Silu, [Bpf[ig]], [Bsg[r]])
                        TT("dve", hT[:, f, :], sg[:, r, :], bank(iu), ALU.mult, [Bsg[r], Bpf[iu]], [BhT])
                    for j in range(4):
                        c = t * 4 + j
                        iz = rotf.get(2)
                        for nb in range(2):
                            for f in range(22):
                                MM(bank(iz + nb), hT[:, f, j * 128:(j + 1) * 128], wd_sb[:, f, nb * 512:(nb + 1) * 512],
                                   f == 0, f == 21, [BhT, Bwd], [Bpf[iz + nb]], inc=(f == 21))
                        TT("dve", hs[:, j, :], hs[:, j, :], bank(iz, 2), ALU.add, [Bhs[j], Bpf[iz], Bpf[iz + 1]], [Bhs[j]])
                        if last:
                            r, Br = rstd_of(hs[:, j, :], Bhs[j])
                            STT("dve", ob[:], hs[:, j, :], r, fn_b[:], ALU.mult, ALU.mult, [Bhs[j], Br, Bcf], [Bob])
                            store_h(out, c, ob[:], Bob)
                        else:
                            store_h(hbuf, c)
                        if c + 4 < NCH:
                            load_h(hbuf, c + 4)

        stages = []
        for l in range(depth):
            stages += [("mix", l), ("att", l), ("ffn", l)]
        for si, (kind, l) in enumerate(stages):
            is_last = (si == len(stages) - 1) or (stop_after == si)
            if si > 0:
                fw.barrier()
            if kind == "mix":
                mixer(l, x if l == 0 else hbuf)
            elif kind == "att":
                attention(l)
            else:
                ffn(l, last=(si == len(stages) - 1))
            if stop_after == si and si != len(stages) - 1:
                for c in range(NCH):
                    load_h(hbuf, c)
                    store_h(out, c)
                break
        fw.finish(Bh + Bhs)
        build.stats = (fw.nops, fw.nwaits, fw.nsem)
    return nc


def host_consts():
    idn = np.eye(128, dtype=np.float32)
    k = np.arange(128)[:, None]
    s = np.arange(128)[None, :]
    U = (k > s).astype(np.float32)
    tri = (k <= s).astype(np.float32)
    ones = np.ones((128, 128), np.float32)
    mats = [idn, U, tri, ones]
    P, Pp, P0 = [], [], []
    sidx = np.arange(128)[:, None]
    tidx = np.arange(128)[None, :]
    for w in (2, 4, 8, 16):
        p = ((sidx <= tidx) & (sidx > tidx - w)).astype(np.float32) / w - (sidx == tidx)
        pp = ((sidx - 128) > (tidx - w)).astype(np.float32) / w
        cnt = np.minimum(tidx + 1, w).astype(np.float32)
        p0 = ((sidx <= tidx) & (sidx > tidx - w)).astype(np.float32) / cnt - (sidx == tidx)
        P.append(p.astype(np.float32)); Pp.append(pp.astype(np.float32)); P0.append(p0.astype(np.float32))
    mats += P + Pp + P0
    mats.append(np.zeros((128, 128), np.float32))
    return np.stack(mats).astype(np.float32)


def host_pvec(inp, depth):
    def T8(v):
        return np.ascontiguousarray(v.reshape(8, 128).T)
    pv = np.zeros((depth, 128, NPV), np.float32)
    for l in range(depth):
        pv[l, :, 0:8] = T8(inp["mix_norm"][l])
        pv[l, :, 8:16] = T8(inp["xattn_norm"][l])
        pv[l, :, 16:24] = T8(inp["ffn_norm"][l])
        pv[l, :, 24:32] = T8(inp["mem_norm"][l])
        pv[l, :, 32:40] = T8(inp["ssd_norm"][l])
        cw = inp["conv_w"][l]
        pv[l, :, 40:88] = cw.reshape(4, 12, 128).transpose(2, 1, 0).reshape(128, 48)
        pv[l, :, 88:100] = inp["conv_b"][l].reshape(12, 128).T
        pv[l, :, 100:108] = T8(inp["pool_scale"][l])
    rowp = np.stack([np.stack([inp["dt_bias"][l], inp["a_log"][l], inp["d_skip"][l]]) for l in range(depth)])
    return pv, np.ascontiguousarray(rowp.astype(np.float32))


_CACHE = {}


def run(inputs, ntok, depth, ncores, stop_after=None):
    inp = {k: np.asarray(v) for k, v in inputs.items()}
    key = (ntok, depth, stop_after)
    if key not in _CACHE:
        _CACHE[key] = build(ntok, depth, stop_after)
    nc = _CACHE[key]
    pv, rowp = host_pvec(inp, depth)
    cm = host_consts()
    in_maps = []
    for c in range(ncores):
        b = c % inp["x"].shape[0]
        m = {
            "x": np.ascontiguousarray(inp["x"][b, :ntok]),
            "mem": np.ascontiguousarray(inp["mem"][b]),
            "pvec": pv, "rowp": rowp, "cmat": cm,
            "final_norm": np.ascontiguousarray(inp["final_norm"]),
        }
        for k in ("w_in", "pool_w", "w_out_mix", "w_q", "w_kv", "w_o", "w_gate_up", "w_down"):
            m[k] = np.ascontiguousarray(inp[k][:depth])
        in_maps.append(m)
    res = run_bass_kernel_spmd(nc, in_maps, core_ids=list(range(ncores)))
    return [r["out"] for r in res.results]


def kernel(**inputs):
    B = np.asarray(inputs["x"]).shape[0]
    outs = run(inputs, SEQ, DEPTH, B)
    return np.stack(outs[:B]).astype(np.float32)
```

```python
import numpy as np
import concourse.bass as bass
import concourse.mybir as mybir
from concourse.bass_utils import run_bass_kernel_spmd
from contextlib import ExitStack

F32 = mybir.dt.float32
BF16 = mybir.dt.bfloat16
AF = mybir.ActivationFunctionType
ALU = mybir.AluOpType

DEPTH = 2
SEQ = 8192
EPS = 1e-6
NPV = 116
import os
DBG = int(os.environ.get('KDBG', '99'))


class Buf:
    __slots__ = ("name", "last_w", "readers", "sem", "semcnt")

    def __init__(self, name):
        self.name = name
        self.last_w = None
        self.readers = []
        self.sem = None
        self.semcnt = 0


class FW:
    def __init__(self, nc, es):
        self.nc = nc
        self.es = es
        self.engs = {"pe": nc.tensor, "dve": nc.vector, "act": nc.scalar,
                     "pool": nc.gpsimd, "sp": nc.sync}
        self.sems = {}
        self.cnt = {}
        for k in self.engs:
            self.sems[k] = es.enter_context(nc.semaphore("s_" + k))
            self.cnt[k] = 0
        self.seen = {k: {} for k in self.engs}
        self.nsem = len(self.engs)
        self.nwaits = 0
        self.nops = 0
        self.dmabufs = []

    def _wait(self, eng, deps):
        e = self.engs[eng]
        best = {}
        for (k, v) in deps:
            if k == eng and eng == "pe":
                continue
            if best.get(k, 0) < v:
                best[k] = v
        for k, v in best.items():
            if self.seen[eng].get(k, 0) < v:
                e.wait_ge(self.sems[k], v)
                self.seen[eng][k] = v
                self.nwaits += 1

    def _deps(self, reads, writes):
        deps = []
        for b in reads:
            if b.last_w is not None:
                deps.append(b.last_w)
        for b in writes:
            if b.last_w is not None:
                deps.append(b.last_w)
            deps.extend(b.readers)
        return deps

    def _record(self, ident, reads, writes):
        for b in reads:
            b.readers.append(ident)
            if len(b.readers) > 48:
                best = {}
                for k, v in b.readers:
                    if best.get(k, 0) < v:
                        best[k] = v
                b.readers = list(best.items())
        for b in writes:
            b.last_w = ident
            b.readers = []

    def op(self, eng, fn, reads=(), writes=(), inc=True):
        self._wait(eng, self._deps(reads, writes))
        inst = fn()
        self.nops += 1
        if inc:
            self.cnt[eng] += 1
            inst.then_inc(self.sems[eng], 1)
            ident = (eng, self.cnt[eng])
        else:
            ident = (eng, self.cnt[eng] + 1)
        self._record(ident, reads, writes)
        return inst

    def dma(self, eng, out, in_, reads=(), writes=(), sembuf=None, **kw):
        if sembuf.sem is None:
            key = "d%d" % self.nsem
            sembuf.sem = key
            self.dmabufs.append(sembuf)
            self.sems[key] = self.es.enter_context(self.nc.semaphore(key))
            self.nsem += 1
        self._wait(eng, self._deps(reads, writes))
        inst = self.engs[eng].dma_start(out=out, in_=in_, **kw)
        sembuf.semcnt += 16
        inst.then_inc(self.sems[sembuf.sem], 16)
        ident = (sembuf.sem, sembuf.semcnt)
        self._record(ident, reads, writes)
        return inst

    def barrier(self):
        deps = [(k, self.cnt[k]) for k in self.engs if self.cnt[k] > 0]
        deps += [(b.sem, b.semcnt) for b in self.dmabufs]
        for eng in self.engs:
            self._wait(eng, deps)

    def finish(self, bufs):
        deps = []
        for b in bufs:
            if b.last_w is not None:
                deps.append(b.last_w)
            deps.extend(b.readers)
        self._wait("sp", deps)


class Rot:
    def __init__(self, n):
        self.n = n
        self.p = 0

    def get(self, k=1):
        if self.p + k > self.n:
            self.p = 0
        r = self.p
        self.p += k
        if self.p >= self.n:
            self.p = 0
        return r


def build(ntok, depth, stop_after=None):
    nc = bass.Bass("TRN2", target_bir_lowering=False)
    NCH = ntok // 128
    NT = ntok // 512

    def din(name, shape):
        return nc.dram_tensor(name, shape, F32, kind="ExternalInput").ap()

    x = din("x", [ntok, 1024])
    mem = din("mem", [256, 1024])
    w_in = din("w_in", [depth, 1024, 3600])
    pool_w = din("pool_w", [depth, 4, 256, 256])
    w_out = din("w_out_mix", [depth, 2048, 1024])
    w_q = din("w_q", [depth, 1024, 1024])
    w_kv = din("w_kv", [depth, 1024, 2048])
    w_o = din("w_o", [depth, 1024, 1024])
    w_gu = din("w_gate_up", [depth, 1024, 5632])
    w_dn = din("w_down", [depth, 2816, 1024])
    pvec = din("pvec", [depth, 128, NPV])
    rowp = din("rowp", [depth, 3, 16])
    fnorm = din("final_norm", [1024])
    cmat = din("cmat", [17, 128, 128])
    out = nc.dram_tensor("out", [ntok, 1024], F32, kind="ExternalOutput").ap()
    hbuf = nc.dram_tensor("hbuf", [ntok, 1024], F32).ap()

    with ExitStack() as es:
        fw = FW(nc, es)

        uniq = [0]

        def sb(st, name, shape, dt):
            uniq[0] += 1
            return st.enter_context(nc.sbuf_tensor("%s_%d" % (name, uniq[0]), shape, dt))

        def MM(out_, lhsT, rhs, st, sp, R, W, inc=False):
            fw.op("pe", lambda: nc.tensor.matmul(out_, lhsT, rhs, start=st, stop=sp), R, W, inc=inc)

        def TR(out_, in_, idn, R, W, inc=False):
            fw.op("pe", lambda: nc.tensor.transpose(out_, in_, idn), R, W, inc=inc)

        def ACT(out_, in_, func, R, W, **kw):
            fw.op("act", lambda: nc.scalar.activation(out=out_, in_=in_, func=func, **kw), R, W)

        def TT(eng, out_, in0, in1, op, R, W):
            e = fw.engs[eng]
            fw.op(eng, lambda: e.tensor_tensor(out=out_, in0=in0, in1=in1, op=op), R, W)

        def TS(eng, out_, in0, s1, op0, R, W):
            e = fw.engs[eng]
            fw.op(eng, lambda: e.tensor_scalar(out=out_, in0=in0, scalar1=s1, scalar2=None, op0=op0), R, W)

        def STT(eng, out_, in0, scalar, in1, op0, op1, R, W):
            e = fw.engs[eng]
            fw.op(eng, lambda: e.scalar_tensor_tensor(out=out_, in0=in0, scalar=scalar, in1=in1, op0=op0, op1=op1), R, W)

        def CP(eng, out_, in_, R, W):
            e = fw.engs[eng]
            fw.op(eng, lambda: e.tensor_copy(out=out_, in_=in_), R, W)

        def MS(eng, ap, val, W):
            e = fw.engs[eng]
            fw.op(eng, lambda: e.memset(ap, val), (), W)

        pf = es.enter_context(nc.psum_tensor("pf", [128, 6 * 512], F32))
        pb = es.enter_context(nc.psum_tensor("pb", [128, 2 * 1024], BF16))
        Bpf = [Buf("pf%d" % i) for i in range(6)]
        Bpb = [Buf("pb%d" % i) for i in range(2)]
        rotf = Rot(6)
        rotb = Rot(2)

        def bank(i, n=1):
            return pf[:, i * 512:(i + n) * 512]

        def bbank(i):
            return pb[:, i * 1024:(i + 1) * 1024]

        cst = es
        ident_b = sb(cst, "ident_b", [128, 128], BF16)
        ones_b = sb(cst, "ones_b", [128, 128], BF16)
        pmat_b = sb(cst, "pmat_b", [128, 12, 128], BF16)
        cf = sb(cst, "cf", [128, 3, 128], F32)
        fn_b = sb(cst, "fn_b", [128, 1024], F32)
        Bc = Buf("consts")
        fw.dma("pool", ident_b[:], cmat[0], writes=[Bc], sembuf=Bc)
        fw.dma("pool", ones_b[:], cmat[3], writes=[Bc], sembuf=Bc)
        for i in range(12):
            fw.dma("pool", pmat_b[:, i, :], cmat[4 + i], writes=[Bc], sembuf=Bc)
        Bcf = Buf("constsf")
        for i in range(3):
            fw.dma("sp", cf[:, i, :], cmat[1 + i], writes=[Bcf], sembuf=Bcf)
        fw.dma("sp", fn_b[:], fnorm.partition_broadcast(128), writes=[Bcf], sembuf=Bcf)
        Umat = cf[:, 0, :]
        tri = cf[:, 1, :]
        onesf = cf[:, 2, :]

        pv = sb(cst, "pv", [128, depth, NPV], F32)
        rp = sb(cst, "rp", [128, depth, 48], F32)
        Bpv = Buf("pv")
        for l in range(depth):
            fw.dma("sp", pv[:, l, :], pvec[l], writes=[Bpv], sembuf=Bpv)
            for i in range(3):
                fw.dma("sp", rp[:, l, i * 16:(i + 1) * 16], rowp[l, i].partition_broadcast(128), writes=[Bpv], sembuf=Bpv)
            ACT(rp[:, l, 16:32], rp[:, l, 16:32], AF.Exp, [Bpv], [Bpv])
            TS("dve", rp[:, l, 16:32], rp[:, l, 16:32], -1.0, ALU.mult, [Bpv], [Bpv])
        PV_MIX, PV_XAT, PV_FFN, PV_MEM, PV_SSD, PV_CW, PV_CB, PV_PS = 0, 8, 16, 24, 32, 40, 88, 100

        hs = sb(cst, "hs", [128, 4, 1024], F32)
        Bhs = [Buf("hs%d" % i) for i in range(4)]
        Bh = [Buf("h%d" % c) for c in range(NCH)]
        ub2 = sb(cst, "ub", [128, 2, 1024], BF16)
        Bub2 = [Buf("ub0"), Buf("ub1")]
        rotu = Rot(2)
        junk = sb(cst, "junk", [128, 1024], BF16)
        Bjunk = Buf("junk")
        ssr = sb(cst, "ssr", [128, 4, 4], F32)
        Bss = [Buf("ss%d" % i) for i in range(4)]
        rots = Rot(4)
        uT = sb(cst, "uT", [128, 8, 512], BF16)
        BuT = Buf("uT")

        def rstd_of(src, Bsrc):
            i = rots.get()
            s = ssr[:, i, :]
            MS("pool", s[:, 0:1], 0.0, [Bss[i]])
            ACT(junk[:], src, AF.Square, [Bsrc, Bss[i]], [Bjunk, Bss[i]], accum_out=s[:, 0:1])
            ACT(s[:, 1:2], s[:, 0:1], AF.Ln, [Bss[i]], [Bss[i]], scale=1.0 / 1024, bias=EPS)
            ACT(s[:, 2:3], s[:, 1:2], AF.Exp, [Bss[i]], [Bss[i]], scale=-0.5)
            return s[:, 2:3], Bss[i]

        def norm_transpose(src, Bsrc, gT, dst, Bdst, tokoff):
            r, Br = rstd_of(src, Bsrc)
            u = rotu.get()
            ACT(ub2[:, u, :], src, AF.Identity, [Bsrc, Br], [Bub2[u]], scale=r)
            transpose8(ub2[:, u, :], Bub2[u], gT, dst, Bdst, tokoff)

        def transpose8(srcb, Bsrcb, gT, dst, Bdst, tokoff):
            i = rotb.get()
            for k in range(8):
                TR(bbank(i)[:, k * 128:(k + 1) * 128], srcb[:, k * 128:(k + 1) * 128], ident_b[:],
                   [Bsrcb, Bc], [Bpb[i]], inc=(k == 7))
            TT("dve", dst[:, 0:8, tokoff:tokoff + 128],
               bbank(i).rearrange("p (q t) -> p q t", t=128),
               gT.unsqueeze(2).to_broadcast([128, 8, 128]),
               ALU.mult, [Bpb[i], Bpv], [Bdst])

        def load_h(src, c):
            j = c % 4
            fw.dma("sp", hs[:, j, :], src[c * 128:(c + 1) * 128, :], reads=[Bh[c]], writes=[Bhs[j]], sembuf=Bhs[j])

        def store_h(dst, c, srcap=None, Bsrc=None):
            j = c % 4
            if srcap is None:
                srcap, Bsrc = hs[:, j, :], Bhs[j]
            fw.dma("sp", dst[c * 128:(c + 1) * 128, :], srcap, reads=[Bsrc], writes=[Bh[c]], sembuf=Bsrc)

        def load_w_cols(dst3, src2, nk, blocks):
            for Bb, ranges in blocks:
                for k in range(nk):
                    for (c0, c1) in ranges:
                        fw.dma("pool", dst3[:, k, c0:c1], src2[k * 128:(k + 1) * 128, c0:c1], writes=[Bb], sembuf=Bb)

        def load_w(dst3, Bdst, src2, nk):
            for k in range(nk):
                if os.environ.get('KNOW'):
                    continue
                fw.dma("pool", dst3[:, k, :], src2[k * 128:(k + 1) * 128, :], writes=[Bdst], sembuf=Bdst)

        def mixer(l, src):
            with ExitStack() as ph:
                w_in_sb = sb(ph, "w_in_sb", [128, 8, 3600], BF16)
                w_out_sb = sb(ph, "w_out_sb", [128, 16, 1024], BF16)
                pool_sb = sb(ph, "pool_sb", [128, 8, 256], BF16)
                Bwout, Bpool = Buf("w_out"), Buf("poolw")
                Bwin_x, Bwin_z, Bwin_v = Buf("w_in_x"), Buf("w_in_z"), Buf("w_in_v")
                for c in range(min(4, NCH)):
                    load_h(src, c)
                for k in range(8):
                    fw.dma("pool", w_in_sb[:, k, :], w_in[l, k * 128:(k + 1) * 128, :],
                           writes=[Bwin_x, Bwin_z, Bwin_v], sembuf=Bwin_x)
                for g in range(4):
                    for cc in range(2):
                        fw.dma("pool", pool_sb[:, g * 2 + cc, :], pool_w[l, g, cc * 128:(cc + 1) * 128, :],
                               writes=[Bpool], sembuf=Bpool)
                load_w(w_out_sb, Bwout, w_out[l], 16)

                raw = sb(ph, "raw", [128, 2, 516], F32)
                Braw = [Buf("raw0"), Buf("raw1")]
                acc = sb(ph, "acc", [128, 2, 512], F32)
                Bacc = [Buf("acc0"), Buf("acc1")]
                halo = sb(ph, "halo", [128, 12, 3], F32)
                Bhalo = Buf("halo")
                xbcT = sb(ph, "xbcT", [128, 12, 512], BF16)
                BxT = Buf("xbcT")
                zs = sb(ph, "zs", [128, 1024], BF16)
                Bzs = Buf("zs")
                vt = sb(ph, "vt", [128, 2, 1024], BF16)
                Bvt = [Buf("vt0"), Buf("vt1")]
                xs_tok = sb(ph, "xs_tok", [128, 1024], BF16)
                Bxs = Buf("xs_tok")
                B_tok = sb(ph, "B_tok", [128, 256], BF16)
                BBt = Buf("B_tok")
                sm = sb(ph, "sm", [128, 8, 16], F32)
                Bsm = Buf("sm")
                Wm = sb(ph, "Wm", [128, 16, 128], F32)
                BWm = Buf("Wm")
                E = sb(ph, "E", [128, 16, 128], BF16)
                BE = Buf("E")
                Mm = sb(ph, "Mm", [128, 16, 128], BF16)
                BM = Buf("Mm")
                CBm = sb(ph, "CBm", [128, 2, 128], F32)
                BCBm = Buf("CBm")
                xdt = sb(ph, "xdt", [128, 1024], BF16)
                Bxdt = Buf("xdt")
                xdd = sb(ph, "xdd", [128, 1024], BF16)
                Bxdd = Buf("xdd")
                xsd = sb(ph, "xsd", [128, 1024], BF16)
                Bxsd = Buf("xsd")
                S = sb(ph, "S", [128, 1024], F32)
                BS = Buf("S")
                S_bf = sb(ph, "S_bf", [128, 1024], BF16)
                BSb = Buf("S_bf")
                yb = sb(ph, "yb", [128, 1024], F32)
                Byb = Buf("yb")
                ygn = sb(ph, "ygn", [128, 1024], BF16)
                Bygn = Buf("ygn")
                catT = sb(ph, "catT", [128, 2, 16, 128], BF16)
                Bcat = [Buf("catT0"), Buf("catT1")]
                dTs = sb(ph, "dTs", [128, 8, 128], BF16)
                BdT = Buf("dTs")

                gT = pv[:, l, PV_MIX:PV_MIX + 8]
                ssdT = pv[:, l, PV_SSD:PV_SSD + 8]
                cw = pv[:, l, PV_CW:PV_CW + 48]
                cb = pv[:, l, PV_CB:PV_CB + 12]
                psT = pv[:, l, PV_PS:PV_PS + 8]
                dtb = rp[:, l, 0:16]
                a_b = rp[:, l, 16:32]
                dsk = rp[:, l, 32:48]

                MS("pool", halo[:], 0.0, [Bhalo])
                MS("pool", S[:], 0.0, [BS])
                MS("pool", S_bf[:], 0.0, [BSb])

                def b16(ap):
                    return ap.unsqueeze(2).to_broadcast([128, 16, 64])

                def v3(ap):
                    return ap.rearrange("p (h j) -> p h j", j=64)

                def step1(j):
                    tk = slice(j * 128, (j + 1) * 128)
                    xr, ax, ex, l1, dt_, adt, dtds, ea = [sm[:, q, :] for q in range(8)]
                    i = rotf.get()
                    for k in range(8):
                        MM(bank(i)[:, 0:16], uT[:, k, tk], w_in_sb[:, k, 2560:2576],
                           k == 0, k == 7, [Bwin_v, BuT], [Bpf[i]], inc=(k == 7))
                    TT("dve", xr, bank(i)[:, 0:16], dtb, ALU.add, [Bpf[i], Bpv], [Bsm])
                    ACT(ax, xr, AF.Abs, [Bsm], [Bsm])
                    ACT(ex, ax, AF.Exp, [Bsm], [Bsm], scale=-1.0)
                    ACT(l1, ex, AF.Ln, [Bsm], [Bsm], bias=1.0)
                    STT("dve", dt_, xr, 0.0, l1, ALU.max, ALU.add, [Bsm], [Bsm])
                    TT("dve", adt, dt_, a_b, ALU.mult, [Bsm, Bpv], [Bsm])
                    TT("pool", Wm[:], tri.unsqueeze(1).to_broadcast([128, 16, 128]),
                       adt.unsqueeze(2).to_broadcast([128, 16, 128]), ALU.mult, [Bcf, Bsm], [BWm])

                pend = [None]

                def outproj(c):
                    j = c % 4
                    cbf = c % 2
                    iz = rotf.get(2)
                    for nb in range(2):
                        for k in range(16):
                            MM(bank(iz + nb), catT[:, cbf, k, :], w_out_sb[:, k, nb * 512:(nb + 1) * 512],
                               k == 0, k == 15, [Bcat[cbf], Bwout], [Bpf[iz + nb]], inc=(k == 15))
                    def tail():
                        TT("dve", hs[:, j, :], hs[:, j, :], bank(iz, 2), ALU.add, [Bhs[j], Bpf[iz], Bpf[iz + 1]], [Bhs[j]])
                        store_h(hbuf, c)
                        if c + 4 < NCH:
                            load_h(src, c + 4)
                    return tail

                def flush_pending(defer_tail=False):
                    if pend[0] is not None:
                        tail = outproj(pend[0])
                        pend[0] = None
                        if defer_tail:
                            return tail
                        tail()
                    return None

                for t in range(NT):
                    flush_pending()
                    for j in range(4):
                        norm_transpose(hs[:, j, :], Bhs[j], gT, uT, BuT, j * 128)
                    def conv_silu(cc):
                        r = cc % 2
                        ACT(xbcT[:, cc, :], acc[:, r, :], AF.Silu, [Bacc[r], Bpv], [BxT], bias=cb[:, cc:cc + 1])

                    for cc in range(12):
                        i = rotf.get()
                        for k in range(8):
                            MM(bank(i), w_in_sb[:, k, 1024 + cc * 128:1024 + (cc + 1) * 128], uT[:, k, :],
                               k == 0, k == 7, [Bwin_x, BuT], [Bpf[i]], inc=(k == 7))
                        r = cc % 2
                        CP("pool", raw[:, r, 0:3], halo[:, cc, :], [Bhalo], [Braw[r]])
                        ACT(raw[:, r, 3:515], bank(i), AF.Identity, [Bpf[i]], [Braw[r]])
                        ACT(acc[:, r, :], bank(i), AF.Identity, [Bpf[i], Bpv], [Bacc[r]], scale=cw[:, cc * 4 + 3:cc * 4 + 4])
                        if cc > 0:
                            conv_silu(cc - 1)
                        CP("pool", halo[:, cc, :], raw[:, r, 512:515], [Braw[r]], [Bhalo])
                        for k in range(3):
                            STT("dve", acc[:, r, :], raw[:, r, k:k + 512], cw[:, cc * 4 + k:cc * 4 + k + 1],
                                acc[:, r, :], ALU.mult, ALU.add, [Braw[r], Bpv, Bacc[r]], [Bacc[r]])
                    conv_silu(11)
                    for j in range(4):
                        c = t * 4 + j
                        tk = slice(j * 128, (j + 1) * 128)
                        cur, prv = c % 2, (c + 1) % 2
                        cbf = c % 2
                        xr, ax, ex, l1, dt_, adt, dtds, ea = [sm[:, q, :] for q in range(8)]
                        if j == 0:
                            step1(j)

                        def step_z():
                            i = rotf.get(2)
                            for nb in range(2):
                                for k in range(8):
                                    MM(bank(i + nb), uT[:, k, tk], w_in_sb[:, k, nb * 512:(nb + 1) * 512],
                                       k == 0, k == 7, [Bwin_z, BuT], [Bpf[i + nb]], inc=(k == 7))
                            ACT(zs[:], bank(i, 2), AF.Silu, [Bpf[i], Bpf[i + 1]], [Bzs])

                        def step_v():
                            i = rotf.get(2)
                            for nb in range(2):
                                for k in range(8):
                                    MM(bank(i + nb), uT[:, k, tk], w_in_sb[:, k, 2576 + nb * 512:2576 + (nb + 1) * 512],
                                       k == 0, k == 7, [Bwin_v, BuT], [Bpf[i + nb]], inc=(k == 7))
                            ACT(vt[:, cur, :], bank(i, 2), AF.Identity, [Bpf[i], Bpf[i + 1]], [Bvt[cur]])

                        def step_pool():
                            for half in range(2):
                                ip = rotf.get()
                                for q in range(4):
                                    cbk = half * 4 + q
                                    g = cbk // 2
                                    if c == 0:
                                        MM(bank(ip)[:, q * 128:(q + 1) * 128], vt[:, cur, cbk * 128:(cbk + 1) * 128],
                                           pmat_b[:, 8 + g, :], True, True, [Bvt[cur], Bc], [Bpf[ip]], inc=(q == 3))
                                    else:
                                        MM(bank(ip)[:, q * 128:(q + 1) * 128], vt[:, cur, cbk * 128:(cbk + 1) * 128],
                                           pmat_b[:, g, :], True, False, [Bvt[cur], Bc], [Bpf[ip]])
                                        MM(bank(ip)[:, q * 128:(q + 1) * 128], vt[:, prv, cbk * 128:(cbk + 1) * 128],
                                           pmat_b[:, 4 + g, :], False, True, [Bvt[prv], Bc], [Bpf[ip]], inc=(q == 3))
                                ACT(dTs[:, half * 4:(half + 1) * 4, :].rearrange("p q t -> p (q t)"), bank(ip), AF.Identity,
                                    [Bpf[ip]], [BdT])
                            ipp = rotf.get(2)
                            for oc in range(8):
                                g = oc // 2
                                dst = bank(ipp + oc // 4)[:, (oc % 4) * 128:(oc % 4 + 1) * 128]
                                for cc2 in range(2):
                                    MM(dst, pool_sb[:, g * 2 + cc2, (oc % 2) * 128:(oc % 2 + 1) * 128], dTs[:, g * 2 + cc2, :],
                                       cc2 == 0, cc2 == 1, [Bpool, BdT], [Bpf[ipp + oc // 4]], inc=(cc2 == 1 and oc % 4 == 3))
                            for hb in range(2):
                                TT("pool" if False else "dve", catT[:, cbf, 8 + hb * 4:8 + (hb + 1) * 4, :],
                                   bank(ipp + hb).rearrange("p (q t) -> p q t", t=128),
                                   psT[:, hb * 4:(hb + 1) * 4].unsqueeze(2).to_broadcast([128, 4, 128]), ALU.mult,
                                   [Bpf[ipp + hb], Bpv], [Bcat[cbf]])

                        if j == 0:
                            step_z()
                            step_v()
                            step_pool()
                        ib = rotb.get()
                        for k in range(8):
                            TR(bbank(ib)[:, k * 128:(k + 1) * 128], xbcT[:, k, tk], ident_b[:],
                               [BxT, Bc], [Bpb[ib]], inc=(k == 7))
                        ACT(xs_tok[:], bbank(ib), AF.Identity, [Bpb[ib]], [Bxs])
                        ib = rotb.get()
                        for q in range(2):
                            TR(bbank(ib)[:, q * 128:(q + 1) * 128], xbcT[:, 8 + q, tk], ident_b[:],
                               [BxT, Bc], [Bpb[ib]], inc=(q == 1))
                        CP("dve", B_tok[:], bbank(ib)[:, 0:256], [Bpb[ib]], [BBt])
                        TT("pool", v3(xsd[:]), v3(xs_tok[:]), b16(dsk), ALU.mult, [Bxs, Bpv], [Bxsd])
                        if j > 0:
                            step_z()
                        ic = rotf.get()
                        for g in range(2):
                            MM(bank(ic)[:, g * 128:(g + 1) * 128], xbcT[:, 8 + g, tk], xbcT[:, 10 + g, tk],
                               True, True, [BxT], [Bpf[ic]], inc=(g == 1))
                        TT("dve", CBm[:], bank(ic)[:, 0:256].rearrange("p (g l) -> p g l", g=2),
                           tri.unsqueeze(1).to_broadcast([128, 2, 128]), ALU.mult, [Bpf[ic], Bcf], [BCBm])
                        i4 = rotf.get(4)
                        for q in range(4):
                            MM(bank(i4 + q), Umat, Wm[:, q * 4:(q + 1) * 4, :].rearrange("p h l -> p (h l)"),
                               True, True, [Bcf, BWm], [Bpf[i4 + q]], inc=True)
                        ia = rotf.get()
                        MM(bank(ia)[:, 0:16], tri, adt, True, True, [Bcf, Bsm], [Bpf[ia]])
                        MM(bank(ia)[:, 16:32], onesf, adt, True, True, [Bcf, Bsm], [Bpf[ia]], inc=True)
                        for q in range(4):
                            ACT(E[:, q * 4:(q + 1) * 4, :].rearrange("p h l -> p (h l)"), bank(i4 + q), AF.Exp,
                                [Bpf[i4 + q]], [BE])
                        cdb = l1
                        ACT(ea, bank(ia)[:, 0:16], AF.Exp, [Bpf[ia]], [Bsm])
                        ACT(cdb, bank(ia)[:, 16:32], AF.Exp, [Bpf[ia]], [Bsm])
                        TT("pool", v3(S[:]), v3(S[:]), b16(cdb), ALU.mult, [BS, Bsm], [BS])
                        CP("dve", xr, bank(ia)[:, 0:16], [Bpf[ia]], [Bsm])
                        TT("dve", ax, bank(ia)[:, 16:32], xr, ALU.subtract, [Bpf[ia], Bsm], [Bsm])
                        ACT(ex, ax, AF.Exp, [Bsm], [Bsm])
                        TT("dve", dtds, dt_, ex, ALU.mult, [Bsm], [Bsm])
                        TT("dve", v3(xdt[:]), v3(xs_tok[:]), b16(dt_), ALU.mult, [Bxs, Bsm], [Bxdt])
                        TT("pool", v3(xdd[:]), v3(xs_tok[:]), b16(dtds), ALU.mult, [Bxs, Bsm], [Bxdd])
                        if j > 0:
                            step_v()
                        for g in range(2):
                            TT("dve" if g == 0 else "pool", Mm[:, g * 8:(g + 1) * 8, :], E[:, g * 8:(g + 1) * 8, :],
                               CBm[:, g, :].unsqueeze(1).to_broadcast([128, 8, 128]), ALU.mult, [BE, BCBm], [BM])
                        if j > 0:
                            step_pool()
                        io = rotf.get(2)
                        for g in range(2):
                            MM(bank(io + g), xbcT[:, 10 + g, tk], S_bf[:, g * 512:(g + 1) * 512], True, True,
                               [BxT, BSb], [Bpf[io + g]], inc=True)
                        TT("dve", v3(yb[:]), bank(io, 2).rearrange("p (h j) -> p h j", j=64), b16(ea), ALU.mult,
                           [Bpf[io], Bpf[io + 1], Bsm], [Byb])
                        ist = rotf.get(2)
                        for g in range(2):
                            MM(bank(ist + g), B_tok[:, g * 128:(g + 1) * 128], xdd[:, g * 512:(g + 1) * 512], True, True,
                               [BBt, Bxdd], [Bpf[ist + g]], inc=True)
                        iy = rotf.get(2)
                        for nb in range(2):
                            MM(bank(iy + nb), ident_b[:], xsd[:, nb * 512:(nb + 1) * 512], True, False,
                               [Bc, Bxsd], [Bpf[iy + nb]])
                            for hh in range(8):
                                h = nb * 8 + hh
                                MM(bank(iy + nb)[:, hh * 64:(hh + 1) * 64], Mm[:, h, :], xdt[:, h * 64:(h + 1) * 64],
                                   False, hh == 7, [BM, Bxdt], [Bpf[iy + nb]], inc=(hh == 7))
                        TT("dve", S[:], S[:], bank(ist, 2), ALU.add, [BS, Bpf[ist], Bpf[ist + 1]], [BS])
                        CP("pool", S_bf[:], S[:], [BS], [BSb])
                        TT("dve", yb[:], yb[:], bank(iy, 2), ALU.add, [Byb, Bpf[iy], Bpf[iy + 1]], [Byb])
                        TT("dve", yb[:], yb[:], zs[:], ALU.mult, [Byb, Bzs], [Byb])
                        if j < 3:
                            step1(j + 1)
                        tail = flush_pending(defer_tail=True)
                        r2, Br2 = rstd_of(yb[:], Byb)
                        TS("dve", ygn[:], yb[:], r2, ALU.mult, [Byb, Br2], [Bygn])
                        transpose8(ygn, Bygn, ssdT, catT[:, cbf], Bcat[cbf], 0)
                        if tail is not None:
                            tail()
                        pend[0] = c
                flush_pending()

        def attention(l):
            with ExitStack() as ph:
                wq_sb = sb(ph, "wq_sb", [128, 8, 1024], BF16)
                wo_sb = sb(ph, "wo_sb", [128, 8, 1024], BF16)
                wkv_sb = sb(ph, "wkv_sb", [128, 8, 1024], BF16)
                Bwq, Bwo, Bwkv = Buf("wq"), Buf("wo"), Buf("wkv")
                memT = sb(ph, "memT", [128, 8, 256], BF16)
                BmT = Buf("memT")
                kT = sb(ph, "kT", [128, 8, 256], BF16)
                BkT = Buf("kT")
                Vs = sb(ph, "Vs", [128, 2, 1024], BF16)
                BV = Buf("Vs")
                qT = sb(ph, "qT", [128, 8, 512], BF16)
                BqT = Buf("qT")
                pT = sb(ph, "pT", [128, 8, 512], BF16)
                BpT = [Buf("pT%d" % i) for i in range(8)]
                rinv = sb(ph, "rinv", [128, 2, 512], F32)
                Bri = [Buf("rinv0"), Buf("rinv1")]
                oTn = sb(ph, "oTn", [128, 8, 512], BF16)
                BoT = Buf("oTn")
                mh = sb(ph, "mh", [128, 1024], F32)
                Bmh = Buf("mh")

                for mc in range(2):
                    fw.dma("sp", mh[:], mem[mc * 128:(mc + 1) * 128, :], writes=[Bmh], sembuf=Bmh)
                    norm_transpose(mh[:], Bmh, pv[:, l, PV_MEM:PV_MEM + 8], memT, BmT, mc * 128)
                for k in range(8):
                    fw.dma("pool", wkv_sb[:, k, :], w_kv[l, k * 128:(k + 1) * 128, 0:1024], writes=[Bwkv], sembuf=Bwkv)
                load_w(wq_sb, Bwq, w_q[l], 8)
                load_w(wo_sb, Bwo, w_o[l], 8)
                for oc in range(8):
                    i = rotf.get()
                    for k in range(8):
                        MM(bank(i)[:, 0:256], wkv_sb[:, k, oc * 128:(oc + 1) * 128], memT[:, k, :], k == 0, k == 7,
                           [Bwkv, BmT], [Bpf[i]], inc=(k == 7))
                    ACT(kT[:, oc, :], bank(i)[:, 0:256], AF.Identity, [Bpf[i]], [BkT])
                for k in range(8):
                    fw.dma("pool", wkv_sb[:, k, :], w_kv[l, k * 128:(k + 1) * 128, 1024:2048], reads=[], writes=[Bwkv], sembuf=Bwkv)
                for mc in range(2):
                    for nb in range(2):
                        i = rotf.get()
                        for k in range(8):
                            MM(bank(i), memT[:, k, mc * 128:(mc + 1) * 128], wkv_sb[:, k, nb * 512:(nb + 1) * 512],
                               k == 0, k == 7, [Bwkv, BmT], [Bpf[i]], inc=(k == 7))
                        ACT(Vs[:, mc, nb * 512:(nb + 1) * 512], bank(i), AF.Identity, [Bpf[i]], [BV])

                gT = pv[:, l, PV_XAT:PV_XAT + 8]
                hsA = sb(ph, "hsA", [128, 8, 1024], F32)
                BhsA = [Buf("hsA%d" % i) for i in range(8)]
                uTA = sb(ph, "uTA", [128, 2, 8, 512], BF16)
                BuTA = [Buf("uTA0"), Buf("uTA1")]
                ubA = sb(ph, "ubA", [128, 4, 1024], BF16)
                BubA = [Buf("ubA%d" % i) for i in range(4)]

                def loadA(c):
                    sl = c % 8
                    fw.dma("sp", hsA[:, sl, :], hbuf[c * 128:(c + 1) * 128, :], reads=[Bh[c]], writes=[BhsA[sl]],
                           sembuf=BhsA[sl])

                def storeA(c):
                    sl = c % 8
                    fw.dma("sp", hbuf[c * 128:(c + 1) * 128, :], hsA[:, sl, :], reads=[BhsA[sl]], writes=[Bh[c]],
                           sembuf=BhsA[sl])

                def normA(t):
                    for j in range(4):
                        c = t * 4 + j
                        r, Br = rstd_of(hsA[:, c % 8, :], BhsA[c % 8])
                        ACT(ubA[:, j, :], hsA[:, c % 8, :], AF.Identity, [BhsA[c % 8], Br], [BubA[j]], scale=r)

                def transA(t, j):
                    transpose8(ubA[:, j, :], BubA[j], gT, uTA[:, t % 2], BuTA[t % 2], j * 128)

                for c in range(min(8, NCH)):
                    loadA(c)
                normA(0)
                for j in range(4):
                    transA(0, j)
                for t in range(NT):
                    u = uTA[:, t % 2]
                    Bu = BuTA[t % 2]
                    for oc in range(8):
                        i = rotf.get()
                        for k in range(8):
                            MM(bank(i), wq_sb[:, k, oc * 128:(oc + 1) * 128], u[:, k, :], k == 0, k == 7,
                               [Bwq, Bu], [Bpf[i]], inc=(k == 7))
                        if oc % 2 == 0:
                            ACT(qT[:, oc, :], bank(i), AF.Identity, [Bpf[i]], [BqT])
                        else:
                            CP("dve", qT[:, oc, :], bank(i), [Bpf[i]], [BqT])
                    for hd in range(4):
                        for mc in range(2):
                            i = rotf.get()
                            for dd in range(2):
                                dc = 2 * hd + dd
                                MM(bank(i), kT[:, dc, mc * 128:(mc + 1) * 128], qT[:, dc, :], dd == 0, dd == 1,
                                   [BkT, BqT], [Bpf[i]], inc=(dd == 1))
                            p = hd * 2 + mc
                            ACT(pT[:, p, :], bank(i), AF.Exp, [Bpf[i]], [BpT[p]], scale=1.0 / 16.0)
                    if t + 1 < NT:
                        normA(t + 1)
                    for hd in range(4):
                        pi = [hd * 2, hd * 2 + 1]
                        i = rotf.get()
                        for mc in range(2):
                            MM(bank(i), ones_b[:], pT[:, pi[mc], :], mc == 0, mc == 1, [Bc, BpT[pi[mc]]], [Bpf[i]], inc=(mc == 1))
                        rr = hd % 2
                        ACT(rinv[:, rr, :], bank(i), AF.Ln, [Bpf[i]], [Bri[rr]])
                        ACT(rinv[:, rr, :], rinv[:, rr, :], AF.Exp, [Bri[rr]], [Bri[rr]], scale=-1.0)
                        for dd in range(2):
                            dc = 2 * hd + dd
                            i2 = rotf.get()
                            for mc in range(2):
                                MM(bank(i2), Vs[:, mc, dc * 128:(dc + 1) * 128], pT[:, pi[mc], :], mc == 0, mc == 1,
                                   [BV, BpT[pi[mc]]], [Bpf[i2]], inc=(mc == 1))
                            TT("dve", oTn[:, dc, :], bank(i2), rinv[:, rr, :], ALU.mult, [Bpf[i2], Bri[rr]], [BoT])
                    for j in range(4):
                        c = t * 4 + j
                        sl = c % 8
                        iz = rotf.get(2)
                        for nb in range(2):
                            for k in range(8):
                                MM(bank(iz + nb), oTn[:, k, j * 128:(j + 1) * 128], wo_sb[:, k, nb * 512:(nb + 1) * 512],
                                   k == 0, k == 7, [BoT, Bwo], [Bpf[iz + nb]], inc=(k == 7))
                        if t + 1 < NT:
                            transA(t + 1, j)
                        TT("dve", hsA[:, sl, :], hsA[:, sl, :], bank(iz, 2), ALU.add, [BhsA[sl], Bpf[iz], Bpf[iz + 1]], [BhsA[sl]])
                        storeA(c)
                        if c + 8 < NCH:
                            loadA(c + 8)

        def ffn(l, last):
            with ExitStack() as ph:
                wgu_sb = sb(ph, "wgu_sb", [128, 8, 5632], BF16)
                wd_sb = sb(ph, "wd_sb", [128, 22, 1024], BF16)
                Bwd = Buf("wd")
                FB = [(0, 6), (6, 12), (12, 17), (17, 22)]
                Bwgu_b = [Buf("wgu%d" % i) for i in range(4)]
                Bwgu_f = {}
                for bi, (f0, f1) in enumerate(FB):
                    for f in range(f0, f1):
                        Bwgu_f[f] = Bwgu_b[bi]
                hT = sb(ph, "hT", [128, 22, 512], BF16)
                BhT = Buf("hT")
                sg = sb(ph, "sg", [128, 2, 512], F32)
                Bsg = [Buf("sg0"), Buf("sg1")]
                ob = sb(ph, "ob", [128, 1024], F32)
                Bob = Buf("ob")
                for c in range(min(4, NCH)):
                    load_h(hbuf, c)
                for k in range(8):
                    fw.dma("pool", wgu_sb[:, k, :], w_gu[l, k * 128:(k + 1) * 128, :], writes=Bwgu_b, sembuf=Bwgu_b[0])
                load_w(wd_sb, Bwd, w_dn[l], 22)
                gT = pv[:, l, PV_FFN:PV_FFN + 8]
                for t in range(NT):
                    for j in range(4):
                        norm_transpose(hs[:, j, :], Bhs[j], gT, uT, BuT, j * 128)
                    for f in range(22):
                        ig = rotf.get()
                        for k in range(8):
                            MM(bank(ig), wgu_sb[:, k, f * 128:(f + 1) * 128], uT[:, k, :], k == 0, k == 7,
                               [Bwgu_f[f], BuT], [Bpf[ig]], inc=(k == 7))
                        iu = rotf.get()
                        for k in range(8):
                            MM(bank(iu), wgu_sb[:, k, 2816 + f * 128:2816 + (f + 1) * 128], uT[:, k, :], k == 0, k == 7,
                               [Bwgu_f[f], BuT], [Bpf[iu]], inc=(k == 7))
                        r = f % 2
                        ACT(sg[:, r, :], bank(ig), AF.Silu, [Bpf[ig]], [Bsg[r]])
                        TT("dve", hT[:, f, :], sg[:, r, :], bank(iu), ALU.mult, [Bsg[r], Bpf[iu]], [BhT])
                    for j in range(4):
                        c = t * 4 + j
                        iz = rotf.get(2)
                        for nb in range(2):
                            for f in range(22):
                                MM(bank(iz + nb), hT[:, f, j * 128:(j + 1) * 128], wd_sb[:, f, nb * 512:(nb + 1) * 512],
                                   f == 0, f == 21, [BhT, Bwd], [Bpf[iz + nb]], inc=(f == 21))
                        TT("dve", hs[:, j, :], hs[:, j, :], bank(iz, 2), ALU.add, [Bhs[j], Bpf[iz], Bpf[iz + 1]], [Bhs[j]])
                        if last:
                            r, Br = rstd_of(hs[:, j, :], Bhs[j])
                            STT("dve", ob[:], hs[:, j, :], r, fn_b[:], ALU.mult, ALU.mult, [Bhs[j], Br, Bcf], [Bob])
                            store_h(out, c, ob[:], Bob)
                        else:
                            store_h(hbuf, c)
                        if c + 4 < NCH:
                            load_h(hbuf, c + 4)

        stages = []
        for l in range(depth):
            stages += [("mix", l), ("att", l), ("ffn", l)]
        for si, (kind, l) in enumerate(stages):
            is_last = (si == len(stages) - 1) or (stop_after == si)
            if si > 0:
                fw.barrier()
            if kind == "mix":
                mixer(l, x if l == 0 else hbuf)
            elif kind == "att":
                attention(l)
            else:
                ffn(l, last=(si == len(stages) - 1))
            if stop_after == si and si != len(stages) - 1:
                for c in range(NCH):
                    load_h(hbuf, c)
                    store_h(out, c)
                break
        fw.finish(Bh + Bhs)
        build.stats = (fw.nops, fw.nwaits, fw.nsem)
    return nc


def host_consts():
    idn = np.eye(128, dtype=np.float32)
    k = np.arange(128)[:, None]
    s = np.arange(128)[None, :]
    U = (k > s).astype(np.float32)
    tri = (k <= s).astype(np.float32)
    ones = np.ones((128, 128), np.float32)
    mats = [idn, U, tri, ones]
    P, Pp, P0 = [], [], []
    sidx = np.arange(128)[:, None]
    tidx = np.arange(128)[None, :]
    for w in (2, 4, 8, 16):
        p = ((sidx <= tidx) & (sidx > tidx - w)).astype(np.float32) / w - (sidx == tidx)
        pp = ((sidx - 128) > (tidx - w)).astype(np.float32) / w
        cnt = np.minimum(tidx + 1, w).astype(np.float32)
        p0 = ((sidx <= tidx) & (sidx > tidx - w)).astype(np.float32) / cnt - (sidx == tidx)
        P.append(p.astype(np.float32)); Pp.append(pp.astype(np.float32)); P0.append(p0.astype(np.float32))
    mats += P + Pp + P0
    mats.append(np.zeros((128, 128), np.float32))
    return np.stack(mats).astype(np.float32)


def host_pvec(inp, depth):
    def T8(v):
        return np.ascontiguousarray(v.reshape(8, 128).T)
    pv = np.zeros((depth, 128, NPV), np.float32)
    for l in range(depth):
        pv[l, :, 0:8] = T8(inp["mix_norm"][l])
        pv[l, :, 8:16] = T8(inp["xattn_norm"][l])
        pv[l, :, 16:24] = T8(inp["ffn_norm"][l])
        pv[l, :, 24:32] = T8(inp["mem_norm"][l])
        pv[l, :, 32:40] = T8(inp["ssd_norm"][l])
        cw = inp["conv_w"][l]
        pv[l, :, 40:88] = cw.reshape(4, 12, 128).transpose(2, 1, 0).reshape(128, 48)
        pv[l, :, 88:100] = inp["conv_b"][l].reshape(12, 128).T
        pv[l, :, 100:108] = T8(inp["pool_scale"][l])
    rowp = np.stack([np.stack([inp["dt_bias"][l], inp["a_log"][l], inp["d_skip"][l]]) for l in range(depth)])
    return pv, np.ascontiguousarray(rowp.astype(np.float32))


_CACHE = {}


def run(inputs, ntok, depth, ncores, stop_after=None):
    inp = {k: np.asarray(v) for k, v in inputs.items()}
    key = (ntok, depth, stop_after)
    if key not in _CACHE:
        _CACHE[key] = build(ntok, depth, stop_after)
    nc = _CACHE[key]
    pv, rowp = host_pvec(inp, depth)
    cm = host_consts()
    in_maps = []
    for c in range(ncores):
        b = c % inp["x"].shape[0]
        m = {
            "x": np.ascontiguousarray(inp["x"][b, :ntok]),
            "mem": np.ascontiguousarray(inp["mem"][b]),
            "pvec": pv, "rowp": rowp, "cmat": cm,
            "final_norm": np.ascontiguousarray(inp["final_norm"]),
        }
        for k in ("w_in", "pool_w", "w_out_mix", "w_q", "w_kv", "w_o", "w_gate_up", "w_down"):
            m[k] = np.ascontiguousarray(inp[k][:depth])
        in_maps.append(m)
    res = run_bass_kernel_spmd(nc, in_maps, core_ids=list(range(ncores)))
    return [r["out"] for r in res.results]


def kernel(**inputs):
    B = np.asarray(inputs["x"]).shape[0]
    outs = run(inputs, SEQ, DEPTH, B)
    return np.stack(outs[:B]).astype(np.float32)
```
